# Optimizing a Trainium2 kernel written in Bass

```python
import math
import jax, jax.numpy as jnp
from jax import lax
import numpy as np

D_MODEL = 2048
BATCH = 2
SEQ = 16384
DEPTH = 2

D_SSM = 1024
SSM_GROUP = 16
N_SSM_GROUPS = D_SSM // SSM_GROUP
SSM_STATE = 64
SSM_DT_MIN = 1e-3
SSM_DT_MAX = 1e-1
GLA_HEADS = 4
GLA_DK = 128
GLA_DV = 256
GLA_GATE_RANK = 16
GLA_GATE_NORM = 16.0
GLA_CHUNK = 32
DIFF_HEADS = 4
DIFF_DK = 128
DIFF_DV = 256
Q_BLOCK = 128
N_BRANCH = 3
D_FF = 5504
CONV_W = 3
EPS = 1e-6
IN_SPLITS = (D_SSM, GLA_HEADS * GLA_DK, GLA_HEADS * GLA_DK, GLA_HEADS * GLA_DV, GLA_HEADS * GLA_DV, GLA_GATE_RANK, DIFF_HEADS * 2 * DIFF_DK, DIFF_HEADS * 2 * DIFF_DK, DIFF_HEADS * DIFF_DV, N_BRANCH * D_MODEL)
D_IN = sum(IN_SPLITS)
ALIBI_SLOPES = tuple(2.0 ** (-8.0 * (h + 1) / DIFF_HEADS) for h in range(DIFF_HEADS))

kernel_name = 'hybrid_s5_gla_diffattn_convffn'


def _rms_f32(x, w):
    xf = x.astype(jnp.float32)
    return xf * lax.rsqrt(jnp.mean(xf * xf, axis=-1, keepdims=True) + EPS) * w.astype(jnp.float32)


def rmsnorm(x, w):
    return _rms_f32(x, w).astype(x.dtype)


def _split_points():
    return tuple(int(c) for c in np.cumsum(IN_SPLITS)[:-1])


def _complex_affine_combine(e1, e2):
    a1r, a1i, b1r, b1i = e1
    a2r, a2i, b2r, b2i = e2
    return (a2r * a1r - a2i * a1i,
            a2r * a1i + a2i * a1r,
            a2r * b1r - a2i * b1i + b2r,
            a2r * b1i + a2i * b1r + b2i)


def s5_branch(u, a_re, a_im, log_dt, b_re, b_im, c_re, c_im, d_skip, w_glu):
    f32 = jnp.float32
    bsz, seq, _ = u.shape
    ug = u.astype(f32).reshape(bsz, seq, N_SSM_GROUPS, SSM_GROUP)
    lam_r = jnp.minimum(a_re.astype(f32), -1e-4)
    lam_i = a_im.astype(f32)
    dt = jnp.exp(log_dt.astype(f32))[:, None]
    mag = jnp.exp(dt * lam_r)
    ab_r = mag * jnp.cos(dt * lam_i)
    ab_i = mag * jnp.sin(dt * lam_i)
    den = lam_r * lam_r + lam_i * lam_i
    f_r = ((ab_r - 1.0) * lam_r + ab_i * lam_i) / den
    f_i = (ab_i * lam_r - (ab_r - 1.0) * lam_i) / den
    br = b_re.astype(f32)
    bi = b_im.astype(f32)
    bb_r = f_r[..., None] * br - f_i[..., None] * bi
    bb_i = f_r[..., None] * bi + f_i[..., None] * br
    x_r = jnp.einsum('bsgc,gpc->bsgp', ug, bb_r)
    x_i = jnp.einsum('bsgc,gpc->bsgp', ug, bb_i)
    shape_a = (1, seq, N_SSM_GROUPS, SSM_STATE)
    a_seq_r = jnp.broadcast_to(ab_r, shape_a)
    a_seq_i = jnp.broadcast_to(ab_i, shape_a)
    _, _, h_r, h_i = lax.associative_scan(_complex_affine_combine, (a_seq_r, a_seq_i, x_r, x_i), axis=1)
    y = (jnp.einsum('bsgp,gcp->bsgc', h_r, c_re.astype(f32))
         - jnp.einsum('bsgp,gcp->bsgc', h_i, c_im.astype(f32)))
    y = y + d_skip.astype(f32).reshape(N_SSM_GROUPS, SSM_GROUP) * ug
    z = jax.nn.gelu(y.reshape(bsz, seq, D_SSM)).astype(u.dtype) @ w_glu
    val, gate = jnp.split(z, 2, axis=-1)
    return (val * jax.nn.sigmoid(gate)).astype(u.dtype)


def gla_branch(q, k, v, r, gate_lr, w_gate, b_gate, norm_w, w_br):
    f32 = jnp.float32
    bsz, seq, _ = q.shape
    n = seq // GLA_CHUNK
    cshape_k = (bsz, n, GLA_CHUNK, GLA_HEADS, GLA_DK)
    qf = q.astype(f32).reshape(cshape_k) * (GLA_DK ** -0.5)
    kf = k.astype(f32).reshape(cshape_k)
    vf = v.astype(f32).reshape(bsz, n, GLA_CHUNK, GLA_HEADS, GLA_DV)
    log_a = jax.nn.log_sigmoid(gate_lr.astype(f32) @ w_gate.astype(f32) + b_gate.astype(f32)) / GLA_GATE_NORM
    cum = jnp.cumsum(log_a.reshape(cshape_k), axis=2)
    last = cum[:, :, -1:]
    q_dec = qf * jnp.exp(cum)
    k_inv = kf * jnp.exp(-cum)
    k_tail = kf * jnp.exp(last - cum)
    tri = jnp.tril(jnp.ones((GLA_CHUNK, GLA_CHUNK), dtype=bool))
    scores = jnp.where(tri, jnp.einsum('bnchd,bnshd->bnhcs', q_dec, k_inv), 0.0)
    o_intra = jnp.einsum('bnhcs,bnshv->bnchv', scores, vf)

    def step(state, inp):
        qd, kt, vv, dl = inp
        o = jnp.einsum('bchd,bhdv->bchv', qd, state)
        state = dl[..., None] * state + jnp.einsum('bchd,bchv->bhdv', kt, vv)
        return state, o

    xs = (jnp.moveaxis(q_dec, 1, 0), jnp.moveaxis(k_tail, 1, 0), jnp.moveaxis(vf, 1, 0),
          jnp.moveaxis(jnp.exp(last[:, :, 0]), 1, 0))
    init = jnp.zeros((bsz, GLA_HEADS, GLA_DK, GLA_DV), f32)
    _, o_inter = lax.scan(step, init, xs)
    o = (o_intra + jnp.moveaxis(o_inter, 0, 1)).reshape(bsz, seq, GLA_HEADS, GLA_DV)
    o = _rms_f32(o, norm_w).reshape(bsz, seq, GLA_HEADS * GLA_DV) * jax.nn.silu(r.astype(f32))
    return o.astype(q.dtype) @ w_br


def diff_attn_branch(q, k, v, lam_q1, lam_k1, lam_q2, lam_k2, norm_w, w_br, lambda_init):
    f32 = jnp.float32
    bsz, seq, _ = q.shape
    q = q.reshape(bsz, seq, DIFF_HEADS, 2, DIFF_DK)
    k = k.reshape(bsz, seq, DIFF_HEADS, 2, DIFF_DK)
    v = v.reshape(bsz, seq, DIFF_HEADS, DIFF_DV)
    lam = (jnp.exp(jnp.sum(lam_q1.astype(f32) * lam_k1.astype(f32)))
           - jnp.exp(jnp.sum(lam_q2.astype(f32) * lam_k2.astype(f32))) + lambda_init)
    slopes = jnp.array(ALIBI_SLOPES, dtype=f32)[None, :, None, None, None]
    scale = DIFF_DK ** -0.5
    pos = jnp.arange(seq, dtype=jnp.int32)
    outs = []
    for blk in range(seq // Q_BLOCK):
        start = blk * Q_BLOCK
        end = start + Q_BLOCK
        qb = q[:, start:end]
        kb = k[:, :end]
        vb = v[:, :end]
        s = jnp.einsum('bqhmd,bkhmd->bhmqk', qb, kb).astype(f32) * scale
        rel = (pos[start:end, None] - pos[None, :end]).astype(f32)
        s = s - slopes * rel
        s = jnp.where(rel >= 0.0, s, -jnp.inf)
        p = jax.nn.softmax(s, axis=-1)
        w = p[:, :, 0] - lam * p[:, :, 1]
        outs.append(jnp.einsum('bhqk,bkhv->bqhv', w.astype(v.dtype), vb))
    o = jnp.concatenate(outs, axis=1)
    o = _rms_f32(o, norm_w) * (1.0 - lambda_init)
    return o.reshape(bsz, seq, DIFF_HEADS * DIFF_DV).astype(q.dtype) @ w_br


def conv_gated_mlp(h, w_up, conv_w, conv_b, w_down):
    seq = h.shape[1]
    u = h @ w_up
    pad = jnp.pad(u, ((0, 0), (CONV_W - 1, 0), (0, 0)))
    acc = conv_b
    for i in range(CONV_W):
        acc = acc + pad[:, i:i + seq] * conv_w[i]
    val, gate = jnp.split(acc, 2, axis=-1)
    return (jax.nn.silu(gate) * val) @ w_down


def setup_inputs(seed: int = 0) -> dict:
    key = jax.random.key(seed)
    ks = jax.random.split(key, 32)
    f32 = jnp.float32
    L = DEPTH
    G = N_SSM_GROUPS
    P = SSM_STATE

    def nrm(k, shape, scale):
        return jax.random.normal(k, shape, f32) * scale

    def gain(k, n):
        return 1.0 + nrm(k, (L, n), 0.02)

    return {
        'x': nrm(ks[0], (BATCH, SEQ, D_MODEL), 1.0),
        'norm_mix_w': gain(ks[1], D_MODEL),
        'w_in': nrm(ks[2], (L, D_MODEL, D_IN), D_MODEL ** -0.5),
        'ssm_a_re': -0.5 + nrm(ks[3], (L, G, P), 0.01),
        'ssm_a_im': jnp.pi * jnp.arange(P, dtype=f32) + nrm(ks[4], (L, G, P), 0.01),
        'ssm_log_dt': jax.random.uniform(ks[5], (L, G), f32, math.log(SSM_DT_MIN), math.log(SSM_DT_MAX)),
        'ssm_b_re': nrm(ks[6], (L, G, P, SSM_GROUP), (2 * SSM_GROUP) ** -0.5),
        'ssm_b_im': nrm(ks[7], (L, G, P, SSM_GROUP), (2 * SSM_GROUP) ** -0.5),
        'ssm_c_re': nrm(ks[8], (L, G, SSM_GROUP, P), (2 * P) ** -0.5),
        'ssm_c_im': nrm(ks[9], (L, G, SSM_GROUP, P), (2 * P) ** -0.5),
        'ssm_d': nrm(ks[10], (L, D_SSM), 1.0),
        'ssm_w_glu': nrm(ks[11], (L, D_SSM, 2 * D_MODEL), D_SSM ** -0.5),
        'gla_w_gate': nrm(ks[12], (L, GLA_GATE_RANK, GLA_HEADS * GLA_DK), GLA_GATE_RANK ** -0.5),
        'gla_b_gate': nrm(ks[13], (L, GLA_HEADS * GLA_DK), 0.1),
        'gla_norm_w': gain(ks[14], GLA_DV),
        'gla_w_br': nrm(ks[15], (L, GLA_HEADS * GLA_DV, D_MODEL), (GLA_HEADS * GLA_DV) ** -0.5),
        'diff_lam_q1': nrm(ks[16], (L, DIFF_DK), 0.1),
        'diff_lam_k1': nrm(ks[17], (L, DIFF_DK), 0.1),
        'diff_lam_q2': nrm(ks[18], (L, DIFF_DK), 0.1),
        'diff_lam_k2': nrm(ks[19], (L, DIFF_DK), 0.1),
        'diff_norm_w': gain(ks[20], DIFF_DV),
        'diff_w_br': nrm(ks[21], (L, DIFF_HEADS * DIFF_DV, D_MODEL), (DIFF_HEADS * DIFF_DV) ** -0.5),
        'w_out': nrm(ks[22], (L, D_MODEL, D_MODEL), D_MODEL ** -0.5),
        'norm_ffn_w': gain(ks[23], D_MODEL),
        'ffn_w_up': nrm(ks[24], (L, D_MODEL, 2 * D_FF), D_MODEL ** -0.5),
        'ffn_conv_w': nrm(ks[25], (L, CONV_W, 2 * D_FF), 0.5),
        'ffn_conv_b': nrm(ks[26], (L, 2 * D_FF), 0.01),
        'ffn_w_down': nrm(ks[27], (L, D_FF, D_MODEL), D_FF ** -0.5),
        'norm_final_w': 1.0 + nrm(ks[28], (D_MODEL,), 0.02),
    }


def reference(x, norm_mix_w, w_in, ssm_a_re, ssm_a_im, ssm_log_dt, ssm_b_re, ssm_b_im, ssm_c_re, ssm_c_im,
              ssm_d, ssm_w_glu, gla_w_gate, gla_b_gate, gla_norm_w, gla_w_br, diff_lam_q1, diff_lam_k1,
              diff_lam_q2, diff_lam_k2, diff_norm_w, diff_w_br, w_out, norm_ffn_w, ffn_w_up, ffn_conv_w,
              ffn_conv_b, ffn_w_down, norm_final_w):
    bsz, seq, _ = x.shape
    split_points = _split_points()
    h = x
    for l in range(DEPTH):
        lambda_init = 0.8 - 0.6 * math.exp(-0.3 * l)
        hn = rmsnorm(h, norm_mix_w[l])
        proj = hn @ w_in[l]
        (u_ssm, q_gla, k_gla, v_gla, r_gla, a_lr,
         q_diff, k_diff, v_diff, gates) = jnp.split(proj, split_points, axis=-1)
        y_a = s5_branch(u_ssm, ssm_a_re[l], ssm_a_im[l], ssm_log_dt[l], ssm_b_re[l], ssm_b_im[l],
                        ssm_c_re[l], ssm_c_im[l], ssm_d[l], ssm_w_glu[l])
        y_b = gla_branch(q_gla, k_gla, v_gla, r_gla, a_lr, gla_w_gate[l], gla_b_gate[l],
                         gla_norm_w[l], gla_w_br[l])
        y_c = diff_attn_branch(q_diff, k_diff, v_diff, diff_lam_q1[l], diff_lam_k1[l], diff_lam_q2[l],
                               diff_lam_k2[l], diff_norm_w[l], diff_w_br[l], lambda_init)
        g = jax.nn.sigmoid(gates.astype(jnp.float32)).reshape(bsz, seq, N_BRANCH, D_MODEL)
        merged = (g[:, :, 0] * y_a.astype(jnp.float32) + g[:, :, 1] * y_b.astype(jnp.float32)
                  + g[:, :, 2] * y_c.astype(jnp.float32))
        h = h + merged.astype(h.dtype) @ w_out[l]
        h = h + conv_gated_mlp(rmsnorm(h, norm_ffn_w[l]), ffn_w_up[l], ffn_conv_w[l],
                               ffn_conv_b[l], ffn_w_down[l])
    return rmsnorm(h, norm_final_w)
```

```python
import math
from contextlib import ExitStack

import numpy as np
import ml_dtypes
import concourse.bass as bass
import concourse.mybir as mybir
from concourse.bass_utils import run_bass_kernel_spmd

F32 = mybir.dt.float32
BF16 = mybir.dt.bfloat16
I32 = mybir.dt.int32
AF = mybir.ActivationFunctionType
ALU = mybir.AluOpType
AX = mybir.AxisListType

D = 2048
KC = D // 128
EPS = 1e-6
NCOL_M = 1808
D_FF = 5504
NFC = D_FF // 128
TWO_PI = 2.0 * math.pi

ENGS = ("pe", "act", "dve", "pool", "sp")
DBG = {}


class Res:
    __slots__ = ("name", "lw", "rd")

    def __init__(self, name=""):
        self.name = name
        self.lw = None
        self.rd = []


class Op:
    __slots__ = ("eng", "idx", "fn", "waits", "signal", "kn", "dma", "cnt")

    def __init__(self, eng, idx, fn, dma=None):
        self.eng = eng
        self.idx = idx
        self.fn = fn
        self.waits = []
        self.signal = False
        self.kn = None
        self.dma = dma
        self.cnt = None


class Em:
    def __init__(self, nc, stack, n_dma_sems=None):
        self.nc = nc
        n_dma_sems = n_dma_sems or {"sp": 40, "pool": 24, "act": 8}
        self.csem = {e: stack.enter_context(nc.semaphore("c_" + e)) for e in ("pe", "act", "dve", "pool")}
        self.dsem = {q: [stack.enter_context(nc.semaphore(f"d_{q}{i}")) for i in range(n)]
                     for q, n in n_dma_sems.items()}
        self.dcount = {q: 0 for q in n_dma_sems}
        self.dlast = {q: [None] * n for q, n in n_dma_sems.items()}
        self.ops = {e: [] for e in ENGS}
        self.nsig = {e: 0 for e in ENGS}
        self.known = {e: {f: -1 for f in ENGS} for e in ENGS}
        self.kdma = {e: {} for e in ENGS}
        self.base = {e: 0 for e in ENGS}

    def _need(self, op, prod, kind="raw"):
        if prod is None or prod is op:
            return
        e = op.eng
        if prod.dma is not None:
            key = prod.dma[0]
            if self.kdma[e].get(key, 0) >= prod.dma[1]:
                return
            self.kdma[e][key] = prod.dma[1]
            op.waits.append(("d", key, prod.dma[1]))
            return
        f = prod.eng
        if f == e and op.dma is None:
            if e == "pe" or (kind != "raw" and e != "pool"):
                return
        if self.known[e][f] >= prod.idx:
            return
        prod.signal = True
        op.waits.append(("c", prod))
        self.known[e][f] = prod.idx
        if prod.kn is not None:
            for g, v in prod.kn.items():
                if v > self.known[e][g]:
                    self.known[e][g] = v

    def _track(self, op, r, w):
        for res in r:
            self._need(op, res.lw, "raw")
        for res in w:
            self._need(op, res.lw, "waw")
            for rd in res.rd:
                self._need(op, rd, "war")
        for res in r:
            if op.dma is None:
                res.rd = [x for x in res.rd if not (x.dma is None and x.eng == op.eng)]
            res.rd.append(op)
        for res in w:
            res.lw = op
            res.rd = []
        op.kn = dict(self.known[op.eng])

    def op(self, eng, fn, r=(), w=()):
        o = Op(eng, self.base[eng] + len(self.ops[eng]), fn)
        self._track(o, r, w)
        self.ops[eng].append(o)
        return o

    def dma(self, q, out, in_, r=(), w=(), **kw):
        n = self.dcount[q]
        ns = len(self.dsem[q])
        si = n % ns
        val = 16 * (n // ns + 1)
        self.dcount[q] = n + 1
        o = Op(q, self.base[q] + len(self.ops[q]), None, dma=((q, si), val))
        o.fn = (out, in_, kw)
        prev = self.dlast[q][si]
        if prev is not None:
            self._need(o, prev)
        self.dlast[q][si] = o
        self._track(o, r, w)
        self.ops[q].append(o)
        return o

    def flush(self, barrier=True):
        nc = self.nc
        if barrier:
            lasts = []
            for e in ("pe", "act", "dve", "pool"):
                c = [o for o in self.ops[e] if o.dma is None]
                if c:
                    c[-1].signal = True
                    lasts.append(c[-1])
            dl = [o for q in self.dlast for o in self.dlast[q] if o is not None]
            for e in ENGS:
                o = Op(e, self.base[e] + len(self.ops[e]), "nop")
                for p in lasts:
                    if p.eng != e:
                        self._need(o, p)
                for p in dl:
                    self._need(o, p)
                o.kn = dict(self.known[e])
                self.ops[e].append(o)
        for e in ENGS:
            c = self.nsig[e]
            for o in self.ops[e]:
                if o.signal:
                    c += 1
                    o.cnt = c
            self.nsig[e] = c
        engobj = {"pe": "tensor", "act": "scalar", "dve": "vector", "pool": "gpsimd", "sp": "sync"}
        with nc.Block() as block:
            for e in ENGS:
                ops = self.ops[e]
                if not ops:
                    continue

                def body(eng, ops=ops, e=e):
                    for o in ops:
                        for wt in o.waits:
                            if wt[0] == "d":
                                (q, si), val = wt[1], wt[2]
                                eng.wait_ge(self.dsem[q][si], val)
                            else:
                                p = wt[1]
                                eng.wait_ge(self.csem[p.eng], p.cnt)
                        if o.dma is not None:
                            out, in_, kw = o.fn
                            (q, si), val = o.dma
                            eng.dma_start(out=out, in_=in_, **kw).then_inc(self.dsem[q][si], 16)
                        elif o.fn == "nop":
                            pass
                        else:
                            ins = o.fn(eng)
                            if o.signal:
                                ins.then_inc(self.csem[e], 1)

                getattr(block, engobj[e])(body)
        for e in ENGS:
            self.base[e] += len(self.ops[e])
            self.ops[e] = []


class Ctx:
    def __init__(self, nc, stack):
        self.nc = nc
        self.st = stack
        self.em = Em(nc, stack)
        self.n = 0

    def sb(self, st, name, shape, dt):
        self.n += 1
        t = st.enter_context(self.nc.sbuf_tensor(f"{name}_{self.n}", list(shape), dt))
        return t, Res(name)

    def ps(self, st, name, shape, dt):
        self.n += 1
        t = st.enter_context(self.nc.psum_tensor(f"{name}_{self.n}", list(shape), dt))
        return t, Res(name)

    def dram(self, name, shape, dt, kind="Internal"):
        return self.nc.dram_tensor(name, list(shape), dt, kind=kind).ap(), Res(name)


def make_ident(cx, st):
    ident, r = cx.sb(st, "ident", [128, 128], BF16)
    cx.em.op("pool", lambda e: e.memset(ident[:], 0.0), w=[r])
    cx.em.op("pool", lambda e: e.affine_select(out=ident[:], in_=ident[:], pattern=[[-1, 128]],
                                               compare_op=ALU.not_equal, fill=1.0, base=0,
                                               channel_multiplier=1), r=[r], w=[r])
    return ident, r


def norm_transpose(cx, src, rsrc, dstT, rdstT, col0, nwb, rnw, bufs, ident, rident, rows=128):
    em = cx.em
    (junk, rjunk), (ss, rss), (hb, rhb), tps = bufs
    R = rows
    em.op("act", lambda e: e.activation(out=junk[0:R, :], in_=src, func=AF.Square, accum_out=ss[0:R, 0:1]),
          r=[rsrc], w=[rjunk, rss])
    em.op("act", lambda e: e.activation(out=ss[0:R, 1:2], in_=ss[0:R, 0:1], func=AF.Sqrt, scale=1.0 / D, bias=EPS),
          r=[rss], w=[rss])
    em.op("dve", lambda e: e.reciprocal(out=ss[0:R, 2:3], in_=ss[0:R, 1:2]), r=[rss], w=[rss])
    em.op("dve", lambda e: e.scalar_tensor_tensor(out=hb[0:R, :], in0=src, scalar=ss[0:R, 2:3], in1=nwb[0:R, :],
                                                  op0=ALU.mult, op1=ALU.mult),
          r=[rsrc, rss, rnw], w=[rhb])
    for half in range(2):
        tp, rtp = tps[half]
        for j in range(8):
            k = half * 8 + j
            em.op("pe", lambda e, k=k, j=j, tp=tp: e.transpose(out=tp[:, j, 0:R], in_=hb[0:R, k * 128:(k + 1) * 128],
                                                              identity=ident[0:R, 0:R]),
                  r=[rhb, rident], w=[rtp])
        em.op("act", lambda e, half=half, tp=tp: e.copy(
            out=dstT[:, half * 8:(half + 1) * 8, col0:col0 + R], in_=tp[:, :, 0:R]), r=[rtp], w=[rdstT])


def phase_M1(cx, S, io, scr, ident, rident):
    em, nc = cx.em, cx.nc
    NT = S // 512
    with ExitStack() as st:
        wmb, rwmb = cx.sb(st, "wmb", [128, KC, NCOL_M], BF16)
        nwb, rnw = cx.sb(st, "nwb", [128, D], F32)
        hts = [cx.sb(st, f"ht{i}", [128, D], F32) for i in range(3)]
        nbufs = [(cx.sb(st, f"junk{i}", [128, D], BF16), cx.sb(st, f"ss{i}", [128, 4], F32),
                  cx.sb(st, f"hb{i}", [128, D], BF16)) for i in range(2)]
        hTs = [cx.sb(st, f"hT{i}", [128, KC, 512], BF16) for i in range(2)]
        tps = [cx.ps(st, f"tp{i}", [128, 8, 128], BF16) for i in range(2)]
        accs = [cx.ps(st, f"acc{i}", [128, 512], F32) for i in range(5)]
        s32 = [cx.sb(st, f"s32_{i}", [128, 512], F32) for i in range(4)]
        s16 = [cx.sb(st, f"s16_{i}", [128, 512], BF16) for i in range(4)]
        wm_v = io["wm"].rearrange("(k p) n -> p k n", p=128)
        for k in range(KC):
            em.dma("pool", wmb[:, k, :], wm_v[:, k, :], w=[rwmb])
        em.dma("sp", nwb[:], io["nmwb"], w=[rnw])
        cnt = {"acc": 0, "s32": 0, "s16": 0, "sub": 0}

        def next_acc():
            a = accs[cnt["acc"] % len(accs)]
            cnt["acc"] += 1
            return a

        def stage(dt):
            key = "s32" if dt == F32 else "s16"
            pool = s32 if dt == F32 else s16
            b = pool[cnt[key] % len(pool)]
            cnt[key] += 1
            return b

        fm = [
            (0, 128, lambda t0: scr["uT"][0][0:128, t0:t0 + 512], F32, None),
            (128, 128, lambda t0: scr["uT"][0][128:256, t0:t0 + 512], F32, None),
            (256, 128, lambda t0: scr["qgT"][0][:, t0:t0 + 512], F32, None),
            (384, 128, lambda t0: scr["kgT"][0][:, t0:t0 + 512], F32, None),
            (512, 128, lambda t0: scr["qdT"][0][0, :, t0:t0 + 512], BF16, 128.0 ** -0.5),
            (640, 128, lambda t0: scr["qdT"][0][1, :, t0:t0 + 512], BF16, 128.0 ** -0.5),
            (768, 128, lambda t0: scr["kdT"][0][0, :, t0:t0 + 512], BF16, None),
            (896, 128, lambda t0: scr["kdT"][0][1, :, t0:t0 + 512], BF16, None),
            (1792, 16, lambda t0: scr["alrT"][0][:, t0:t0 + 512], F32, None),
        ]
        fm_res = [scr["uT"][1], scr["uT"][1], scr["qgT"][1], scr["kgT"][1], scr["qdT"][1], scr["qdT"][1],
                  scr["kdT"][1], scr["kdT"][1], scr["alrT"][1]]
        for i in range(NT):
            t0 = i * 512
            hT, rhT = hTs[i % 2]
            for s in range(4):
                n = cnt["sub"]
                cnt["sub"] += 1
                ht, rht = hts[n % 3]
                em.dma("sp", ht[:], io["h"][t0 + s * 128:t0 + (s + 1) * 128, :], w=[rht])
                (junk, ss, hb) = nbufs[n % 2]
                norm_transpose(cx, ht[:], rht, hT, rhT, s * 128, nwb, rnw, (junk, ss, hb, tps), ident, rident)
            for bi, (c0, M, dst, dt, scale) in enumerate(fm if DBG.get('fm', 1) else []):
                a, ra = next_acc()
                for k in range(KC):
                    em.op("pe", lambda e, a=a, k=k, c0=c0, M=M, hT=hT: e.matmul(
                        a[0:M, :], wmb[:, k, c0:c0 + M], hT[:, k, :], start=(k == 0), stop=(k == KC - 1)),
                        r=[rwmb, rhT], w=[ra])
                sg, rsg = stage(dt)
                if scale is None:
                    em.op("act", lambda e, a=a, sg=sg, M=M: e.copy(out=sg[0:M, :], in_=a[0:M, :]), r=[ra], w=[rsg])
                else:
                    em.op("act", lambda e, a=a, sg=sg, M=M, scale=scale: e.activation(
                        out=sg[0:M, :], in_=a[0:M, :], func=AF.Copy, scale=scale), r=[ra], w=[rsg])
                em.dma("sp", dst(t0), sg[0:M, :], r=[rsg], w=[fm_res[bi]])
            def evac(eng, out, in_, ra, rsg):
                if eng == "act":
                    em.op("act", lambda e: e.copy(out=out, in_=in_), r=[ra], w=[rsg])
                else:
                    em.op("dve", lambda e: e.tensor_copy(out=out, in_=in_), r=[ra], w=[rsg])

            for s in range(4 if DBG.get('tm', 1) else 0):
                r0 = t0 + s * 128
                e1, e2 = ("act", "act")
                a, ra = next_acc()
                for k in range(KC):
                    em.op("pe", lambda e, a=a, k=k, s=s, hT=hT: e.matmul(
                        a[:, :], hT[:, k, s * 128:(s + 1) * 128], wmb[:, k, 1024:1536], start=(k == 0),
                        stop=(k == KC - 1)), r=[rwmb, rhT], w=[ra])
                sg, rsg = stage(BF16)
                evac(e1, sg[:, 0:256], a[:, 0:256], ra, rsg)
                em.dma("sp", scr["vg"][0][r0:r0 + 128, :], sg[:, 0:256], r=[rsg], w=[scr["vg"][1]])
                sg2, rsg2 = stage(F32)
                evac(e1, sg2[:, 0:256], a[:, 256:512], ra, rsg2)
                em.dma("sp", scr["rg"][0][r0:r0 + 128, :], sg2[:, 0:256], r=[rsg2], w=[scr["rg"][1]])
                a, ra = next_acc()
                for k in range(KC):
                    em.op("pe", lambda e, a=a, k=k, s=s, hT=hT: e.matmul(
                        a[:, 0:256], hT[:, k, s * 128:(s + 1) * 128], wmb[:, k, 1536:1792], start=(k == 0),
                        stop=(k == KC - 1)), r=[rwmb, rhT], w=[ra])
                sg, rsg = stage(BF16)
                evac(e2, sg[:, 0:256], a[:, 0:256], ra, rsg)
                em.dma("sp", scr["vd"][0][r0:r0 + 128, :], sg[:, 0:256], r=[rsg], w=[scr["vd"][1]])
        em.flush()


def phase_M4(cx, S, io, scr, ident, rident):
    em = cx.em
    NQ = S // 512
    NKT = S // 128
    with ExitStack() as st:
        KT, rKT = cx.sb(st, "KT", [128, 2, S], BF16)
        V, rV = cx.sb(st, "V", [128, NKT, 257], BF16)
        QTs = [cx.sb(st, f"QT{i}", [128, 2, 512], BF16) for i in range(2)]
        alf, ralf = cx.sb(st, "alf", [3, 128], F32)
        arf, rarf = cx.sb(st, "arf", [3, 512], F32)
        al, ral = cx.sb(st, "al", [3, 128], BF16)
        ar, rar = cx.sb(st, "ar", [3, 512], BF16)
        ctab, rct = cx.sb(st, "ctab", [128, 132], F32)
        lv = [cx.sb(st, f"lv{i}", [128, 128], F32) for i in range(4)]
        sm, rsm = cx.sb(st, "sm", [128, 8], F32)
        dnw, rdnw = cx.sb(st, "dnw", [128, 256], F32)
        Ps = [cx.sb(st, f"P{i}", [128, 512], BF16) for i in range(4)]
        ots = [cx.sb(st, f"ot{i}", [128, 257], F32) for i in range(2)]
        o0 = [cx.sb(st, f"o0_{i}", [128, 256], F32) for i in range(4)]
        od, rod = cx.sb(st, "od", [128, 256], F32)
        junk, rjunk = cx.sb(st, "junk", [128, 256], F32)
        st4, rst4 = cx.sb(st, "st4", [128, 8], F32)
        ocb = [cx.sb(st, f"ocb{i}", [128, 256], BF16) for i in range(2)]
        ocs = [cx.sb(st, f"ocs{i}", [128, 2, 128], BF16) for i in range(2)]
        Sps = [cx.ps(st, f"S{i}", [128, 512], F32) for i in range(3)]
        Ops = [cx.ps(st, f"O{i}", [128, 512], F32) for i in range(4)]
        tpp, rtpp = cx.ps(st, "tpp", [128, 2, 128], BF16)
        for m in range(2):
            em.dma("sp", KT[:, m, :], scr["kdT"][0][m], r=[scr["kdT"][1]], w=[rKT])
        vd_v = scr["vd"][0].rearrange("(k p) v -> p k v", p=128)
        for k0 in range(0, NKT, 16):
            k1 = min(NKT, k0 + 16)
            em.dma("sp", V[:, k0:k1, 0:256], vd_v[:, k0:k1, :], r=[scr["vd"][1]], w=[rV])
        em.op("pool", lambda e: e.memset(V[:, :, 256:257], 1.0), w=[rV])
        em.dma("sp", alf[:], io["al"], w=[ralf])
        em.dma("sp", arf[:], io["ar"], w=[rarf])
        em.op("act", lambda e: e.copy(out=al[:], in_=alf[:]), r=[ralf], w=[ral])
        em.op("act", lambda e: e.copy(out=ar[:], in_=arf[:]), r=[rarf], w=[rar])
        em.dma("sp", ctab[:], io["ctab"], w=[rct])
        em.dma("sp", dnw[:], io["dnw"], w=[rdnw])
        for i, nm in enumerate(("lq1", "lk1", "lq2", "lk2")):
            em.dma("sp", lv[i][0][:], io[nm], w=[lv[i][1]])
        em.dma("sp", sm[:, 4:6], io["lconst"], w=[rsm])
        for j in range(2):
            a, b = lv[2 * j], lv[2 * j + 1]
            em.op("dve", lambda e, a=a, b=b: e.tensor_tensor(out=a[0][:], in0=a[0][:], in1=b[0][:], op=ALU.mult),
                  r=[a[1], b[1]], w=[a[1]])
            em.op("dve", lambda e, a=a, j=j: e.reduce_sum(out=sm[:, j:j + 1], in_=a[0][:], axis=AX.X), r=[a[1]], w=[rsm])
            em.op("act", lambda e, j=j: e.activation(out=sm[:, j + 2:j + 3], in_=sm[:, j:j + 1], func=AF.Exp), r=[rsm], w=[rsm])
        em.op("dve", lambda e: e.tensor_tensor(out=sm[:, 6:7], in0=sm[:, 3:4], in1=sm[:, 2:3], op=ALU.subtract), r=[rsm], w=[rsm])
        em.op("dve", lambda e: e.tensor_tensor(out=sm[:, 6:7], in0=sm[:, 6:7], in1=sm[:, 4:5], op=ALU.subtract), r=[rsm], w=[rsm])
        em.op("dve", lambda e: e.tensor_scalar(out=dnw[:], in0=dnw[:], scalar1=sm[:, 5:6], scalar2=None, op0=ALU.mult),
              r=[rdnw, rsm], w=[rdnw])
        cnt = {"S": 0, "P": 0, "ot": 0, "oc": 0}
        for qi in range(NQ):
            t0 = qi * 512
            QT, rQT = QTs[qi % 2]
            for m in range(2):
                em.dma("sp", QT[:, m, :], scr["qdT"][0][m, :, t0:t0 + 512], r=[scr["qdT"][1]], w=[rQT])
            for m in range(2):
                nkt = 4 * qi + 4
                pend = None

                def pv(kt, P, rP):
                    r = kt - 4 * qi
                    for qb in range(4):
                        if r > qb:
                            continue
                        O, rO = Ops[qb]
                        em.op("pe", lambda e, O=O, P=P, qb=qb, kt=kt, last=(kt == 4 * qi + qb): e.matmul(
                            O[:, 0:257], P[:, qb * 128:(qb + 1) * 128], V[:, kt, :], start=(kt == 0),
                            stop=last), r=[rP, rV], w=[rO])

                for kt in range(nkt):
                    Sp, rSp = Sps[cnt["S"] % 3]
                    cnt["S"] += 1
                    P, rP = Ps[cnt["P"] % 4]
                    cnt["P"] += 1
                    em.op("pe", lambda e, Sp=Sp, m=m, kt=kt, QT=QT: e.matmul(
                        Sp[:, :], KT[:, m, kt * 128:(kt + 1) * 128], QT[:, m, :], start=True, stop=False),
                        r=[rKT, rQT], w=[rSp])
                    em.op("pe", lambda e, Sp=Sp: e.matmul(Sp[:, :], al[:, :], ar[:, :], start=False, stop=True),
                          r=[ral, rar], w=[rSp])
                    if pend is not None:
                        pv(*pend)
                    ci = 4 * qi - kt + 3
                    em.op("act", lambda e, Sp=Sp, P=P, ci=ci: e.activation(
                        out=P[:, :], in_=Sp[:, :], func=AF.Exp, bias=ctab[:, ci:ci + 1]), r=[rSp, rct], w=[rP])
                    r = kt - 4 * qi
                    if r >= 0:
                        em.op("pool", lambda e, P=P, r=r: e.affine_select(
                            out=P[:, :], in_=P[:, :], pattern=[[1, 512]], compare_op=ALU.is_ge, fill=0.0,
                            base=-128 * r, channel_multiplier=-1), r=[rP], w=[rP])
                    pend = (kt, P, rP)
                pv(*pend)
                for qb in range(4):
                    O, rO = Ops[qb]
                    ot, rot = ots[cnt["ot"] % 2]
                    cnt["ot"] += 1
                    em.op("act", lambda e, O=O, ot=ot: e.copy(out=ot[:, :], in_=O[:, 0:257]), r=[rO], w=[rot])
                    em.op("dve", lambda e, ot=ot: e.reciprocal(out=ot[:, 256:257], in_=ot[:, 256:257]), r=[rot], w=[rot])
                    o0t, ro0 = o0[qb]
                    if m == 0:
                        em.op("dve", lambda e, ot=ot, o0t=o0t: e.tensor_scalar(
                            out=o0t[:, :], in0=ot[:, 0:256], scalar1=ot[:, 256:257], scalar2=None, op0=ALU.mult),
                            r=[rot], w=[ro0])
                        continue
                    em.op("dve", lambda e, ot=ot: e.tensor_scalar(
                        out=ot[:, 0:256], in0=ot[:, 0:256], scalar1=ot[:, 256:257], scalar2=None, op0=ALU.mult),
                        r=[rot], w=[rot])
                    em.op("dve", lambda e, ot=ot, o0t=o0t: e.scalar_tensor_tensor(
                        out=od[:, :], in0=ot[:, 0:256], scalar=sm[:, 6:7], in1=o0t[:, :], op0=ALU.mult, op1=ALU.add),
                        r=[rot, ro0, rsm], w=[rod])
                    em.op("act", lambda e: e.activation(out=junk[:, :], in_=od[:, :], func=AF.Square,
                                                        accum_out=st4[:, 0:1]), r=[rod], w=[rjunk, rst4])
                    em.op("act", lambda e: e.activation(out=st4[:, 1:2], in_=st4[:, 0:1], func=AF.Sqrt,
                                                        scale=1.0 / 256, bias=EPS), r=[rst4], w=[rst4])
                    em.op("dve", lambda e: e.reciprocal(out=st4[:, 2:3], in_=st4[:, 1:2]), r=[rst4], w=[rst4])
                    oc, roc = ocb[cnt["oc"] % 2]
                    osg, rosg = ocs[cnt["oc"] % 2]
                    cnt["oc"] += 1
                    em.op("dve", lambda e, oc=oc: e.scalar_tensor_tensor(
                        out=oc[:, :], in0=od[:, :], scalar=st4[:, 2:3], in1=dnw[:, :], op0=ALU.mult, op1=ALU.mult),
                        r=[rod, rst4, rdnw], w=[roc])
                    for c in range(2):
                        em.op("pe", lambda e, oc=oc, c=c: e.transpose(out=tpp[:, c, :], in_=oc[:, c * 128:(c + 1) * 128],
                                                                      identity=ident[:]), r=[roc, rident], w=[rtpp])
                    em.op("act", lambda e, osg=osg: e.copy(out=osg[:, :, :], in_=tpp[:, :, :]), r=[rtpp], w=[rosg])
                    tq = t0 + qb * 128
                    em.dma("sp", io["ocT"].rearrange("(c p) t -> p c t", p=128)[:, :, tq:tq + 128], osg[:, :, :],
                           r=[rosg], w=[io["ocT_res"]])
        em.flush()


def alibi_tables(slope):
    al = np.zeros((3, 128), np.float32)
    al[0] = slope * np.arange(128)
    al[1:] = 1.0
    ar = np.zeros((3, 512), np.float32)
    tj = np.arange(512)
    ar[0] = 1.0
    ar[1] = -slope * 128.0 * (tj // 128)
    ar[2] = -slope * (tj % 128)
    ct = -slope * 128.0 * (np.arange(132) - 3.0)
    ctab = np.ascontiguousarray(np.broadcast_to(ct.astype(np.float32), (128, 132)))
    return al, ar, ctab


def phase_M3(cx, S, io, scr, ident, rident):
    em = cx.em
    NT = S // 512
    with ExitStack() as st:
        wg, rwg = cx.sb(st, "wg", [16, 128], F32)
        bg, rbg = cx.sb(st, "bg", [128, 1], F32)
        gnw, rgnw = cx.sb(st, "gnw", [128, 256], F32)
        resetm, rrm = cx.sb(st, "resetm", [128, 512], F32)
        Sf, rSf = cx.sb(st, "Sf", [128, 256], F32)
        Sb, rSb = cx.sb(st, "Sb", [128, 256], BF16)
        alr = [cx.sb(st, f"alr{i}", [16, 512], F32) for i in range(2)]
        qTs = [cx.sb(st, f"qT{i}", [128, 512], F32) for i in range(2)]
        kTs = [cx.sb(st, f"kT{i}", [128, 512], F32) for i in range(2)]
        vs = [cx.sb(st, f"v{i}", [128, 4, 256], BF16) for i in range(2)]
        rs = [cx.sb(st, f"r{i}", [128, 4, 256], F32) for i in range(2)]
        z, rz = cx.sb(st, "z", [128, 512], F32)
        az, raz = cx.sb(st, "az", [128, 512], F32)
        la, rla = cx.sb(st, "la", [128, 512], F32)
        cum, rcum = cx.sb(st, "cum", [128, 512], F32)
        ecp, recp = cx.sb(st, "ecp", [128, 512], F32)
        ecn, recn = cx.sb(st, "ecn", [128, 512], F32)
        ect, rect = cx.sb(st, "ect", [128, 512], F32)
        lst, rlst = cx.sb(st, "lst", [128, 12], F32)
        qdec, rqd = cx.sb(st, "qdec", [128, 512], BF16)
        kinv, rki = cx.sb(st, "kinv", [128, 512], BF16)
        ktl, rktl = cx.sb(st, "ktl", [128, 512], BF16)
        ktT, rktT = cx.sb(st, "ktT", [128, 4, 128], BF16)
        sTs = [cx.sb(st, f"sT{i}", [128, 128], BF16) for i in range(2)]
        kvs, rkvs = cx.sb(st, "kvs", [128, 256], F32)
        ots = [cx.sb(st, f"ot{i}", [128, 256], F32) for i in range(2)]
        junk, rjunk = cx.sb(st, "junk", [128, 256], F32)
        st4, rst4 = cx.sb(st, "st4", [128, 4], F32)
        gs = [cx.sb(st, f"g{i}", [128, 256], F32) for i in range(2)]
        obb = [cx.sb(st, f"obb{i}", [128, 256], BF16) for i in range(2)]
        osgs = [cx.sb(st, f"osg{i}", [128, 2, 128], BF16) for i in range(2)]
        x_ps, rx = cx.ps(st, "x_ps", [128, 512], F32)
        tpk, rtpk = cx.ps(st, "tpk", [128, 4, 128], BF16)
        sT_ps = [cx.ps(st, f"sTp{i}", [128, 128], F32) for i in range(2)]
        o_ps = [cx.ps(st, f"op{i}", [128, 256], F32) for i in range(2)]
        kv_ps, rkv = cx.ps(st, "kvp", [128, 256], F32)
        tpp, rtpp = cx.ps(st, "tpp", [128, 2, 128], BF16)
        em.dma("sp", wg[:], io["wg"], w=[rwg])
        em.dma("sp", bg[:], io["bg"], w=[rbg])
        em.dma("sp", gnw[:], io["gnw"], w=[rgnw])
        em.op("pool", lambda e: e.memset(resetm[:], 1.0), w=[rrm])
        for c in range(4):
            em.op("pool", lambda e, c=c: e.memset(resetm[:, c * 128:c * 128 + 1], 0.0), w=[rrm])
        em.op("pool", lambda e: e.memset(Sf[:], 0.0), w=[rSf])
        em.op("pool", lambda e: e.memset(Sb[:], 0.0), w=[rSb])
        obT_v = io["obT"].rearrange("(c p) t -> p c t", p=128)
        n = 0
        for i in range(NT):
            t0 = i * 512
            (al_, ral_), (qT, rqT), (kT, rkT), (v, rv), (r_, rr_) = alr[i % 2], qTs[i % 2], kTs[i % 2], vs[i % 2], rs[i % 2]
            em.dma("sp", al_[:], scr["alrT"][0][:, t0:t0 + 512], r=[scr["alrT"][1]], w=[ral_])
            em.dma("sp", qT[:], scr["qgT"][0][:, t0:t0 + 512], r=[scr["qgT"][1]], w=[rqT])
            em.dma("sp", kT[:], scr["kgT"][0][:, t0:t0 + 512], r=[scr["kgT"][1]], w=[rkT])
            em.dma("sp", v[:], scr["vg"][0][t0:t0 + 512, :].rearrange("(s p) v -> p s v", p=128), r=[scr["vg"][1]], w=[rv])
            em.dma("sp", r_[:], scr["rg"][0][t0:t0 + 512, :].rearrange("(s p) v -> p s v", p=128), r=[scr["rg"][1]], w=[rr_])
            em.op("pe", lambda e, al_=al_: e.matmul(x_ps[:, :], wg[:, :], al_[:, :], start=True, stop=True),
                  r=[rwg, ral_], w=[rx])
            em.op("act", lambda e: e.activation(out=z[:], in_=x_ps[:], func=AF.Identity, bias=bg[:, 0:1]),
                  r=[rx, rbg], w=[rz])
            em.op("act", lambda e: e.activation(out=az[:], in_=z[:], func=AF.Abs), r=[rz], w=[raz])
            em.op("act", lambda e: e.activation(out=az[:], in_=az[:], func=AF.Exp, scale=-1.0), r=[raz], w=[raz])
            em.op("act", lambda e: e.activation(out=az[:], in_=az[:], func=AF.Ln, bias=1.0), r=[raz], w=[raz])
            em.op("dve", lambda e: e.scalar_tensor_tensor(out=la[:], in0=z[:], scalar=0.0, in1=az[:], op0=ALU.min,
                                                          op1=ALU.subtract), r=[rz, raz], w=[rla])
            em.op("dve", lambda e: e.tensor_tensor_scan(out=cum[:], data0=resetm[:], data1=la[:], initial=0.0,
                                                        op0=ALU.mult, op1=ALU.add), r=[rrm, rla], w=[rcum])
            for c in range(4):
                em.op("dve", lambda e, c=c: e.tensor_scalar(out=lst[:, c:c + 1], in0=cum[:, c * 128 + 127:c * 128 + 128],
                                                            scalar1=1.0 / 16, scalar2=None, op0=ALU.mult),
                      r=[rcum], w=[rlst])
            em.op("act", lambda e: e.activation(out=lst[:, 4:8], in_=lst[:, 0:4], func=AF.Exp), r=[rlst], w=[rlst])
            em.op("act", lambda e: e.activation(out=ecp[:], in_=cum[:], func=AF.Exp, scale=1.0 / 16), r=[rcum], w=[recp])
            em.op("act", lambda e: e.activation(out=ecn[:], in_=cum[:], func=AF.Exp, scale=-1.0 / 16), r=[rcum], w=[recn])
            for c in range(4):
                em.op("act", lambda e, c=c: e.activation(out=ect[:, c * 128:(c + 1) * 128], in_=cum[:, c * 128:(c + 1) * 128],
                                                         func=AF.Exp, scale=-1.0 / 16, bias=lst[:, c:c + 1]),
                      r=[rcum, rlst], w=[rect])
            em.op("dve", lambda e, qT=qT: e.scalar_tensor_tensor(out=qdec[:], in0=qT[:], scalar=128.0 ** -0.5, in1=ecp[:],
                                                                 op0=ALU.mult, op1=ALU.mult), r=[rqT, recp], w=[rqd])
            em.op("dve", lambda e, kT=kT: e.tensor_tensor(out=kinv[:], in0=kT[:], in1=ecn[:], op=ALU.mult),
                  r=[rkT, recn], w=[rki])
            em.op("dve", lambda e, kT=kT: e.tensor_tensor(out=ktl[:], in0=kT[:], in1=ect[:], op=ALU.mult),
                  r=[rkT, rect], w=[rktl])
            for c in range(4):
                em.op("pe", lambda e, c=c: e.transpose(out=tpk[:, c, :], in_=ktl[:, c * 128:(c + 1) * 128], identity=ident[:]),
                      r=[rktl, rident], w=[rtpk])
            em.op("act", lambda e: e.copy(out=ktT[:], in_=tpk[:]), r=[rtpk], w=[rktT])
            for c in range(4):
                cs = slice(c * 128, (c + 1) * 128)
                sp_, rsp_ = sT_ps[n % 2]
                sT, rsT = sTs[n % 2]
                op_, rop_ = o_ps[n % 2]
                ot, rot = ots[n % 2]
                g, rg_ = gs[n % 2]
                ob, rob = obb[n % 2]
                osg, rosg = osgs[n % 2]
                n += 1
                em.op("pe", lambda e, sp_=sp_, cs=cs: e.matmul(sp_[:, :], kinv[:, cs], qdec[:, cs], start=True, stop=True),
                      r=[rki, rqd], w=[rsp_])
                em.op("act", lambda e, sT=sT, sp_=sp_: e.copy(out=sT[:], in_=sp_[:]), r=[rsp_], w=[rsT])
                em.op("pool", lambda e, sT=sT: e.affine_select(out=sT[:], in_=sT[:], pattern=[[1, 128]],
                                                               compare_op=ALU.is_ge, fill=0.0, base=0,
                                                               channel_multiplier=-1), r=[rsT], w=[rsT])
                em.op("pe", lambda e, op_=op_, sT=sT, v=v, c=c: e.matmul(op_[:, :], sT[:, :], v[:, c, :], start=True, stop=False),
                      r=[rsT, rv], w=[rop_])
                em.op("pe", lambda e, op_=op_, cs=cs: e.matmul(op_[:, :], qdec[:, cs], Sb[:, :], start=False, stop=True),
                      r=[rqd, rSb], w=[rop_])
                em.op("pe", lambda e, v=v, c=c: e.matmul(kv_ps[:, :], ktT[:, c, :], v[:, c, :], start=True, stop=True),
                      r=[rktT, rv], w=[rkv])
                em.op("act", lambda e: e.copy(out=kvs[:], in_=kv_ps[:]), r=[rkv], w=[rkvs])
                em.op("dve", lambda e, c=c: e.scalar_tensor_tensor(out=Sf[:], in0=Sf[:], scalar=lst[:, 4 + c:5 + c], in1=kvs[:],
                                                                   op0=ALU.mult, op1=ALU.add), r=[rSf, rlst, rkvs], w=[rSf])
                em.op("pool", lambda e: e.tensor_copy(out=Sb[:], in_=Sf[:]), r=[rSf], w=[rSb])
                em.op("act", lambda e, ot=ot, op_=op_: e.copy(out=ot[:], in_=op_[:]), r=[rop_], w=[rot])
                em.op("act", lambda e, ot=ot: e.activation(out=junk[:], in_=ot[:], func=AF.Square, accum_out=st4[:, 0:1]),
                      r=[rot], w=[rjunk, rst4])
                em.op("act", lambda e: e.activation(out=st4[:, 1:2], in_=st4[:, 0:1], func=AF.Sqrt, scale=1.0 / 256, bias=EPS),
                      r=[rst4], w=[rst4])
                em.op("dve", lambda e: e.reciprocal(out=st4[:, 2:3], in_=st4[:, 1:2]), r=[rst4], w=[rst4])
                em.op("act", lambda e, g=g, r_=r_, c=c: e.activation(out=g[:], in_=r_[:, c, :], func=AF.Silu), r=[rr_], w=[rg_])
                em.op("pool", lambda e, g=g: e.tensor_tensor(out=g[:], in0=g[:], in1=gnw[:], op=ALU.mult), r=[rg_, rgnw], w=[rg_])
                em.op("dve", lambda e, ob=ob, ot=ot, g=g: e.scalar_tensor_tensor(
                    out=ob[:], in0=ot[:], scalar=st4[:, 2:3], in1=g[:], op0=ALU.mult, op1=ALU.mult),
                    r=[rot, rst4, rg_], w=[rob])
                for h2 in range(2):
                    em.op("pe", lambda e, ob=ob, h2=h2: e.transpose(out=tpp[:, h2, :], in_=ob[:, h2 * 128:(h2 + 1) * 128],
                                                                    identity=ident[:]), r=[rob, rident], w=[rtpp])
                em.op("act", lambda e, osg=osg: e.copy(out=osg[:], in_=tpp[:]), r=[rtpp], w=[rosg])
                tq = t0 + c * 128
                em.dma("sp", obT_v[:, :, tq:tq + 128], osg[:], r=[rosg], w=[io["obT_res"]])
        em.flush()


def _sin_reduced(cx, st_bufs, x, rx, out, rout, shape_cols, phase=0.0):
    em = cx.em
    (ni, rni), (nf, rnf), (rr, rrr), (mk, rmk) = st_bufs
    n = shape_cols
    em.op("dve", lambda e: e.tensor_scalar(out=rr[:, 0:n], in0=x, scalar1=phase, scalar2=None, op0=ALU.add), r=[rx], w=[rrr])
    em.op("dve", lambda e: e.tensor_scalar(out=ni[:, 0:n], in0=rr[:, 0:n], scalar1=1.0 / TWO_PI, scalar2=None, op0=ALU.mult),
          r=[rrr], w=[rni])
    em.op("dve", lambda e: e.tensor_copy(out=nf[:, 0:n], in_=ni[:, 0:n]), r=[rni], w=[rnf])
    em.op("dve", lambda e: e.scalar_tensor_tensor(out=rr[:, 0:n], in0=nf[:, 0:n], scalar=-TWO_PI, in1=rr[:, 0:n],
                                                  op0=ALU.mult, op1=ALU.add), r=[rnf, rrr], w=[rrr])
    em.op("dve", lambda e: e.tensor_single_scalar(out=mk[:, 0:n], in_=rr[:, 0:n], scalar=math.pi, op=ALU.is_gt), r=[rrr], w=[rmk])
    em.op("dve", lambda e: e.scalar_tensor_tensor(out=rr[:, 0:n], in0=mk[:, 0:n], scalar=-TWO_PI, in1=rr[:, 0:n],
                                                  op0=ALU.mult, op1=ALU.add), r=[rmk, rrr], w=[rrr])
    em.op("dve", lambda e: e.tensor_single_scalar(out=mk[:, 0:n], in_=rr[:, 0:n], scalar=-math.pi, op=ALU.is_lt), r=[rrr], w=[rmk])
    em.op("dve", lambda e: e.scalar_tensor_tensor(out=rr[:, 0:n], in0=mk[:, 0:n], scalar=TWO_PI, in1=rr[:, 0:n],
                                                  op0=ALU.mult, op1=ALU.add), r=[rmk, rrr], w=[rrr])
    em.op("dve", lambda e: e.tensor_scalar(out=rr[:, 0:n], in0=rr[:, 0:n], scalar1=math.pi, scalar2=-math.pi, op0=ALU.min,
                                           op1=ALU.max), r=[rrr], w=[rrr])
    em.op("act", lambda e: e.activation(out=out, in_=rr[:, 0:n], func=AF.Sin), r=[rrr], w=[rout])


def phase_M2(cx, S, io, scr, ident, rident):
    em = cx.em
    NT = S // 512
    L = 512
    with ExitStack() as st:
        pa = {k: cx.sb(st, k, [128, 8], F32) for k in ("a_re", "a_im", "ldt")}
        pb = {k: cx.sb(st, k, [128, 8, 16], F32) for k in ("b_re", "b_im", "c_re", "c_im")}
        dsk, rdsk = cx.sb(st, "dsk", [128, 2], F32)
        for k in pa:
            em.dma("sp", pa[k][0][:], io[k], w=[pa[k][1]])
        for k in pb:
            em.dma("sp", pb[k][0][:], io[k], w=[pb[k][1]])
        em.dma("sp", dsk[:], io["dsk"], w=[rdsk])
        W, rW = cx.sb(st, "W", [128, 16, 8], F32)
        col = {nm: i for i, nm in enumerate(["lam_r", "dt", "dlr", "th", "mag", "cs", "sn", "abr", "abi", "den", "fr", "fi",
                                              "t0", "t1", "thL", "nfi"])}
        w_ = lambda nm: W[:, col[nm], :]
        sinb = (cx.sb(st, "sr_ni", [128, 512], I32), cx.sb(st, "sr_nf", [128, 512], F32),
                cx.sb(st, "sr_rr", [128, 512], F32), cx.sb(st, "sr_mk", [128, 512], F32))
        a_re, a_im, ldt = pa["a_re"][0], pa["a_im"][0], pa["ldt"][0]
        rall = [rW, pa["a_re"][1], pa["a_im"][1], pa["ldt"][1]]

        def dv(fn):
            em.op("dve", fn, r=rall, w=[rW])

        dv(lambda e: e.tensor_scalar(out=w_("lam_r"), in0=a_re[:], scalar1=-1e-4, scalar2=None, op0=ALU.min))
        em.op("act", lambda e: e.activation(out=w_("dt"), in_=ldt[:], func=AF.Exp), r=rall, w=[rW])
        dv(lambda e: e.tensor_tensor(out=w_("dlr"), in0=w_("lam_r"), in1=w_("dt"), op=ALU.mult))
        dv(lambda e: e.tensor_tensor(out=w_("th"), in0=a_im[:], in1=w_("dt"), op=ALU.mult))
        em.op("act", lambda e: e.activation(out=w_("mag"), in_=w_("dlr"), func=AF.Exp), r=[rW], w=[rW])
        _sin_reduced(cx, sinb, w_("th"), rW, w_("sn"), rW, 8, 0.0)
        _sin_reduced(cx, sinb, w_("th"), rW, w_("cs"), rW, 8, math.pi / 2)
        dv(lambda e: e.tensor_tensor(out=w_("abr"), in0=w_("mag"), in1=w_("cs"), op=ALU.mult))
        dv(lambda e: e.tensor_tensor(out=w_("abi"), in0=w_("mag"), in1=w_("sn"), op=ALU.mult))
        dv(lambda e: e.tensor_tensor(out=w_("den"), in0=w_("lam_r"), in1=w_("lam_r"), op=ALU.mult))
        dv(lambda e: e.tensor_tensor(out=w_("t0"), in0=a_im[:], in1=a_im[:], op=ALU.mult))
        dv(lambda e: e.tensor_tensor(out=w_("den"), in0=w_("den"), in1=w_("t0"), op=ALU.add))
        dv(lambda e: e.reciprocal(out=w_("den"), in_=w_("den")))
        dv(lambda e: e.tensor_scalar(out=w_("t0"), in0=w_("abr"), scalar1=-1.0, scalar2=None, op0=ALU.add))
        dv(lambda e: e.tensor_tensor(out=w_("fr"), in0=w_("t0"), in1=w_("lam_r"), op=ALU.mult))
        dv(lambda e: e.tensor_tensor(out=w_("t1"), in0=w_("abi"), in1=a_im[:], op=ALU.mult))
        dv(lambda e: e.tensor_tensor(out=w_("fr"), in0=w_("fr"), in1=w_("t1"), op=ALU.add))
        dv(lambda e: e.tensor_tensor(out=w_("fr"), in0=w_("fr"), in1=w_("den"), op=ALU.mult))
        dv(lambda e: e.tensor_tensor(out=w_("fi"), in0=w_("abi"), in1=w_("lam_r"), op=ALU.mult))
        dv(lambda e: e.tensor_tensor(out=w_("t1"), in0=w_("t0"), in1=a_im[:], op=ALU.mult))
        dv(lambda e: e.tensor_tensor(out=w_("fi"), in0=w_("fi"), in1=w_("t1"), op=ALU.subtract))
        dv(lambda e: e.tensor_tensor(out=w_("fi"), in0=w_("fi"), in1=w_("den"), op=ALU.mult))
        dv(lambda e: e.tensor_scalar(out=w_("nfi"), in0=w_("fi"), scalar1=-1.0, scalar2=None, op0=ALU.mult))
        dv(lambda e: e.tensor_scalar(out=w_("thL"), in0=w_("th"), scalar1=float(L), scalar2=None, op0=ALU.mult))
        cL, rcL = cx.sb(st, "cL", [128, 8], F32)
        sL, rsL = cx.sb(st, "sL", [128, 8], F32)
        _sin_reduced(cx, sinb, w_("thL"), rW, sL[:], rsL, 8, 0.0)
        _sin_reduced(cx, sinb, w_("thL"), rW, cL[:], rcL, 8, math.pi / 2)
        bbr, rbbr = cx.sb(st, "bbr", [128, 8, 16], F32)
        bbi, rbbi = cx.sb(st, "bbi", [128, 8, 16], F32)
        b_re, b_im, c_re, c_im = (pb[k][0] for k in ("b_re", "b_im", "c_re", "c_im"))
        rb = [pb["b_re"][1], pb["b_im"][1], rW]
        for pr in range(8):
            fr_s = W[:, col["fr"], pr:pr + 1]
            fi_s = W[:, col["fi"], pr:pr + 1]
            nfi_s = W[:, col["nfi"], pr:pr + 1]
            em.op("dve", lambda e, pr=pr, fr_s=fr_s: e.tensor_scalar(out=bbr[:, pr, :], in0=b_re[:, pr, :], scalar1=fr_s,
                                                                    scalar2=None, op0=ALU.mult), r=rb, w=[rbbr])
            em.op("dve", lambda e, pr=pr, nfi_s=nfi_s: e.scalar_tensor_tensor(out=bbr[:, pr, :], in0=b_im[:, pr, :], scalar=nfi_s,
                                                                             in1=bbr[:, pr, :], op0=ALU.mult, op1=ALU.add),
                  r=rb + [rbbr], w=[rbbr])
            em.op("dve", lambda e, pr=pr, fr_s=fr_s: e.tensor_scalar(out=bbi[:, pr, :], in0=b_im[:, pr, :], scalar1=fr_s,
                                                                    scalar2=None, op0=ALU.mult), r=rb, w=[rbbi])
            em.op("dve", lambda e, pr=pr, fi_s=fi_s: e.scalar_tensor_tensor(out=bbi[:, pr, :], in0=b_re[:, pr, :], scalar=fi_s,
                                                                           in1=bbi[:, pr, :], op0=ALU.mult, op1=ALU.add),
                  r=rb + [rbbi], w=[rbbi])
        ZBr, rZBr = cx.sb(st, "ZBr", [128, 8, 128], BF16)
        ZBi, rZBi = cx.sb(st, "ZBi", [128, 8, 128], BF16)
        ZCr, rZCr = cx.sb(st, "ZCr", [128, 8, 128], BF16)
        ZCi, rZCi = cx.sb(st, "ZCi", [128, 8, 128], BF16)
        BR, rBR = cx.sb(st, "BR", [128, 8, 128], BF16)
        BI, rBI = cx.sb(st, "BI", [128, 8, 128], BF16)
        for Z, rZ in ((ZBr, rZBr), (ZBi, rZBi), (ZCr, rZCr), (ZCi, rZCi)):
            em.op("pool", lambda e, Z=Z: e.memset(Z[:], 0.0), w=[rZ])
        for pr in range(8):
            for g2 in range(2):
                ps_ = slice(64 * g2, 64 * g2 + 64)
                c0 = (2 * (pr % 4) + g2) * 16
                em.op("dve", lambda e, pr=pr, ps_=ps_, c0=c0: e.tensor_copy(out=ZBr[ps_, pr, c0:c0 + 16], in_=bbr[ps_, pr, :]),
                      r=[rbbr], w=[rZBr])
                em.op("dve", lambda e, pr=pr, ps_=ps_, c0=c0: e.tensor_copy(out=ZBi[ps_, pr, c0:c0 + 16], in_=bbi[ps_, pr, :]),
                      r=[rbbi], w=[rZBi])
                em.op("dve", lambda e, pr=pr, ps_=ps_, c0=c0: e.tensor_copy(out=ZCr[ps_, pr, c0:c0 + 16], in_=c_re[ps_, pr, :]),
                      r=[pb["c_re"][1]], w=[rZCr])
                em.op("dve", lambda e, pr=pr, ps_=ps_, c0=c0: e.tensor_scalar(out=ZCi[ps_, pr, c0:c0 + 16], in0=c_im[ps_, pr, :],
                                                                             scalar1=-1.0, scalar2=None, op0=ALU.mult),
                      r=[pb["c_im"][1]], w=[rZCi])
        tpz, rtpz = cx.ps(st, "tpz", [128, 8, 128], BF16)
        for Z, rZ, B_, rB_ in ((ZBr, rZBr, BR, rBR), (ZBi, rZBi, BI, rBI)):
            for pr in range(8):
                em.op("pe", lambda e, Z=Z, pr=pr: e.transpose(out=tpz[:, pr, :], in_=Z[:, pr, :], identity=ident[:]),
                      r=[rZ, rident], w=[rtpz])
            em.op("act", lambda e, B_=B_: e.copy(out=B_[:], in_=tpz[:]), r=[rtpz], w=[rB_])
        tau, rtau = cx.sb(st, "tau", [128, L], F32)
        em.op("pool", lambda e: e.iota(tau[:], pattern=[[1, L]], base=0, channel_multiplier=0,
                                       allow_small_or_imprecise_dtypes=True), w=[rtau])
        cosT, rcos = cx.sb(st, "cosT", [128, 8, L], F32)
        sinT, rsin = cx.sb(st, "sinT", [128, 8, L], F32)
        magT, rmag = cx.sb(st, "magT", [128, 8, L], F32)
        arg, rarg = cx.sb(st, "arg", [128, L], F32)
        ones, rones = cx.sb(st, "ones", [128, L], F32)
        em.op("pool", lambda e: e.memset(ones[:], 1.0), w=[rones])
        for pr in range(8):
            em.op("dve", lambda e, pr=pr: e.tensor_scalar(out=arg[:], in0=tau[:], scalar1=W[:, col["th"], pr:pr + 1],
                                                          scalar2=None, op0=ALU.mult), r=[rtau, rW], w=[rarg])
            _sin_reduced(cx, sinb, arg[:], rarg, sinT[:, pr, :], rsin, L, 0.0)
            _sin_reduced(cx, sinb, arg[:], rarg, cosT[:, pr, :], rcos, L, math.pi / 2)
            em.op("dve", lambda e, pr=pr: e.tensor_scalar(out=magT[:, pr, :], in0=ones[:], scalar1=W[:, col["mag"], pr:pr + 1],
                                                          scalar2=None, op0=ALU.mult), r=[rones, rW], w=[rmag])
        init, rinit = cx.sb(st, "init", [128, 16], F32)
        em.op("pool", lambda e: e.memset(init[:], 0.0), w=[rinit])
        uTs = [cx.sb(st, f"uT{i}", [128, 2, L], F32) for i in range(2)]
        ubs = [cx.sb(st, f"ub{i}", [128, 2, L], BF16) for i in range(2)]
        NB = 2
        f32b = lambda nm: [cx.sb(st, f"{nm}{i}", [128, L], F32) for i in range(NB)]
        xr_, xi_, t1_, t2_, t3_, t4_, mr_, mi_, gr_, gi_ = (f32b(nm) for nm in
                                                             ("xr", "xi", "t1", "t2", "t3", "t4", "mr", "mi", "gr", "gi"))
        hr_ = [cx.sb(st, f"hr{i}", [128, L], BF16) for i in range(NB)]
        hi_ = [cx.sb(st, f"hi{i}", [128, L], BF16) for i in range(NB)]
        tny, rtny = cx.sb(st, "tny", [128, 2], F32)
        ys, rys = cx.sb(st, "ys", [128, L], F32)
        yt, ryt = cx.sb(st, "yt", [128, L], F32)
        yab = [cx.sb(st, f"yab{i}", [128, L], BF16) for i in range(2)]
        xps = [cx.ps(st, f"xps{i}", [128, L], F32) for i in range(4)]
        yps = [cx.ps(st, f"yps{i}", [128, L], F32) for i in range(2)]
        n = 0
        for i in range(NT):
            t0 = i * L
            uT, ruT = uTs[i % 2]
            ub, rub = ubs[i % 2]
            for half in range(2):
                em.dma("sp", uT[:, half, :], scr["uT"][0][half * 128:(half + 1) * 128, t0:t0 + L], r=[scr["uT"][1]], w=[ruT])
            em.op("act", lambda e, ub=ub, uT=uT: e.copy(out=ub[:], in_=uT[:]), r=[ruT], w=[rub])
            for pr in range(8):
                half = pr // 4
                b = n % NB
                n += 1
                (xr, rxr), (xi, rxi) = xr_[b], xi_[b]
                (t1, rt1), (t2, rt2), (t3, rt3), (t4, rt4) = t1_[b], t2_[b], t3_[b], t4_[b]
                (mr, rmr), (mi, rmi), (gr, rgr), (gi, rgi) = mr_[b], mi_[b], gr_[b], gi_[b]
                (hr, rhr), (hi, rhi) = hr_[b], hi_[b]
                xp0, rxp0 = xps[(2 * n) % 4]
                xp1, rxp1 = xps[(2 * n + 1) % 4]
                cs_, sn_ = cosT[:, pr, :], sinT[:, pr, :]
                em.op("pe", lambda e, xp0=xp0, pr=pr, ub=ub, half=half: e.matmul(xp0[:], BR[:, pr, :], ub[:, half, :],
                                                                               start=True, stop=True), r=[rBR, rub], w=[rxp0])
                em.op("pe", lambda e, xp1=xp1, pr=pr, ub=ub, half=half: e.matmul(xp1[:], BI[:, pr, :], ub[:, half, :],
                                                                               start=True, stop=True), r=[rBI, rub], w=[rxp1])
                em.op("act", lambda e, xr=xr, xp0=xp0: e.copy(out=xr[:], in_=xp0[:]), r=[rxp0], w=[rxr])
                em.op("act", lambda e, xi=xi, xp1=xp1: e.copy(out=xi[:], in_=xp1[:]), r=[rxp1], w=[rxi])
                em.op("pool", lambda e, t1=t1, xr=xr, cs_=cs_: e.tensor_tensor(out=t1[:], in0=xr[:], in1=cs_, op=ALU.mult), r=[rxr, rcos], w=[rt1])
                em.op("pool", lambda e, t2=t2, xi=xi, sn_=sn_: e.tensor_tensor(out=t2[:], in0=xi[:], in1=sn_, op=ALU.mult), r=[rxi, rsin], w=[rt2])
                em.op("pool", lambda e, t3=t3, xi=xi, cs_=cs_: e.tensor_tensor(out=t3[:], in0=xi[:], in1=cs_, op=ALU.mult), r=[rxi, rcos], w=[rt3])
                em.op("pool", lambda e, t4=t4, xr=xr, sn_=sn_: e.tensor_tensor(out=t4[:], in0=xr[:], in1=sn_, op=ALU.mult), r=[rxr, rsin], w=[rt4])
                em.op("dve", lambda e, mr=mr, t1=t1, t2=t2: e.tensor_tensor(out=mr[:], in0=t1[:], in1=t2[:], op=ALU.add), r=[rt1, rt2], w=[rmr])
                em.op("dve", lambda e, mi=mi, t3=t3, t4=t4: e.tensor_tensor(out=mi[:], in0=t3[:], in1=t4[:], op=ALU.subtract), r=[rt3, rt4], w=[rmi])
                em.op("dve", lambda e, gr=gr, mr=mr, pr=pr: e.tensor_tensor_scan(out=gr[:], data0=magT[:, pr, :], data1=mr[:],
                                                                                initial=init[:, pr:pr + 1], op0=ALU.mult,
                                                                                op1=ALU.add), r=[rmag, rmr, rinit], w=[rgr])
                em.op("dve", lambda e, gi=gi, mi=mi, pr=pr: e.tensor_tensor_scan(out=gi[:], data0=magT[:, pr, :], data1=mi[:],
                                                                                initial=init[:, 8 + pr:9 + pr], op0=ALU.mult,
                                                                                op1=ALU.add), r=[rmag, rmi, rinit], w=[rgi])
                em.op("dve", lambda e, gi=gi, pr=pr: e.tensor_tensor(out=tny[:, 0:1], in0=gi[:, L - 1:L], in1=sL[:, pr:pr + 1], op=ALU.mult),
                      r=[rgi, rsL], w=[rtny])
                em.op("dve", lambda e, gr=gr, pr=pr: e.scalar_tensor_tensor(out=init[:, pr:pr + 1], in0=gr[:, L - 1:L], scalar=cL[:, pr:pr + 1],
                                                                           in1=tny[:, 0:1], op0=ALU.mult, op1=ALU.subtract),
                      r=[rgr, rcL, rtny], w=[rinit])
                em.op("dve", lambda e, gi=gi, pr=pr: e.tensor_tensor(out=tny[:, 1:2], in0=gi[:, L - 1:L], in1=cL[:, pr:pr + 1], op=ALU.mult),
                      r=[rgi, rcL], w=[rtny])
                em.op("dve", lambda e, gr=gr, pr=pr: e.scalar_tensor_tensor(out=init[:, 8 + pr:9 + pr], in0=gr[:, L - 1:L], scalar=sL[:, pr:pr + 1],
                                                                           in1=tny[:, 1:2], op0=ALU.mult, op1=ALU.add),
                      r=[rgr, rsL, rtny], w=[rinit])
                em.op("pool", lambda e, t1=t1, gr=gr, cs_=cs_: e.tensor_tensor(out=t1[:], in0=gr[:], in1=cs_, op=ALU.mult), r=[rgr, rcos], w=[rt1])
                em.op("pool", lambda e, t2=t2, gi=gi, sn_=sn_: e.tensor_tensor(out=t2[:], in0=gi[:], in1=sn_, op=ALU.mult), r=[rgi, rsin], w=[rt2])
                em.op("pool", lambda e, t3=t3, gr=gr, sn_=sn_: e.tensor_tensor(out=t3[:], in0=gr[:], in1=sn_, op=ALU.mult), r=[rgr, rsin], w=[rt3])
                em.op("pool", lambda e, t4=t4, gi=gi, cs_=cs_: e.tensor_tensor(out=t4[:], in0=gi[:], in1=cs_, op=ALU.mult), r=[rgi, rcos], w=[rt4])
                em.op("dve", lambda e, hr=hr, t1=t1, t2=t2: e.tensor_tensor(out=hr[:], in0=t1[:], in1=t2[:], op=ALU.subtract), r=[rt1, rt2], w=[rhr])
                em.op("dve", lambda e, hi=hi, t3=t3, t4=t4: e.tensor_tensor(out=hi[:], in0=t3[:], in1=t4[:], op=ALU.add), r=[rt3, rt4], w=[rhi])
                yp, ryp = yps[half]
                em.op("pe", lambda e, yp=yp, pr=pr, hr=hr: e.matmul(yp[:], ZCr[:, pr, :], hr[:], start=(pr % 4 == 0), stop=False),
                      r=[rZCr, rhr], w=[ryp])
                em.op("pe", lambda e, yp=yp, pr=pr, hi=hi: e.matmul(yp[:], ZCi[:, pr, :], hi[:], start=False, stop=(pr % 4 == 3)),
                      r=[rZCi, rhi], w=[ryp])
                if pr % 4 == 3:
                    ya, rya = yab[half]
                    em.op("act", lambda e, yp=yp: e.copy(out=ys[:], in_=yp[:]), r=[ryp], w=[rys])
                    em.op("dve", lambda e, uT=uT, half=half: e.scalar_tensor_tensor(out=yt[:], in0=uT[:, half, :], scalar=dsk[:, half:half + 1],
                                                                                   in1=ys[:], op0=ALU.mult, op1=ALU.add),
                          r=[ruT, rdsk, rys], w=[ryt])
                    em.op("act", lambda e, ya=ya: e.activation(out=ya[:], in_=yt[:], func=AF.Gelu), r=[ryt], w=[rya])
                    em.dma("sp", io["yaT"][half * 128:(half + 1) * 128, t0:t0 + L], ya[:], r=[rya], w=[io["yaT_res"]])
        em.flush()


def s5_layout(a_re, a_im, ldt, b_re, b_im, c_re, c_im, dsk):
    def st(x):
        return np.ascontiguousarray(x.reshape(8, 2, 64).transpose(1, 2, 0).reshape(128, 8))
    out = {"a_re": st(a_re), "a_im": st(a_im),
           "ldt": st(np.broadcast_to(ldt[:, None], (16, 64))),
           "b_re": np.ascontiguousarray(b_re.reshape(8, 2, 64, 16).transpose(1, 2, 0, 3).reshape(128, 8, 16)),
           "b_im": np.ascontiguousarray(b_im.reshape(8, 2, 64, 16).transpose(1, 2, 0, 3).reshape(128, 8, 16)),
           "c_re": np.ascontiguousarray(c_re.reshape(8, 2, 16, 64).transpose(1, 3, 0, 2).reshape(128, 8, 16)),
           "c_im": np.ascontiguousarray(c_im.reshape(8, 2, 16, 64).transpose(1, 3, 0, 2).reshape(128, 8, 16)),
           "dsk": np.ascontiguousarray(dsk.reshape(2, 128).T)}
    return {k: v.astype(np.float32) for k, v in out.items()}


def phase_P0(cx, wlist):
    em = cx.em
    with ExitStack() as st:
        fs = [cx.sb(st, f"pf{i}", [128, 16, 256], F32) for i in range(3)]
        bs = [cx.sb(st, f"pb{i}", [128, 16, 256], BF16) for i in range(3)]
        n = 0
        for src, dst, rdst in wlist:
            K_, N_ = src.shape
            nk = K_ // 128
            sv = src.rearrange("(k p) n -> p k n", p=128)
            for nb in range(N_ // 256):
                for k0 in range(0, nk, 16):
                    k1 = min(nk, k0 + 16)
                    f, rf = fs[n % 3]
                    b, rb = bs[n % 3]
                    em.dma("sp", f[:, 0:k1 - k0, :], sv[:, k0:k1, nb * 256:(nb + 1) * 256], w=[rf])
                    eng = ("act", "dve", "pool")[n % 3]
                    if eng == "act":
                        em.op("act", lambda e, f=f, b=b, kk=k1 - k0: e.copy(out=b[:, 0:kk, :], in_=f[:, 0:kk, :]), r=[rf], w=[rb])
                    else:
                        em.op(eng, lambda e, f=f, b=b, kk=k1 - k0: e.tensor_copy(out=b[:, 0:kk, :], in_=f[:, 0:kk, :]), r=[rf], w=[rb])
                    em.dma("act", dst[nb, :, k0:k1, :], b[:, 0:k1 - k0, :], r=[rb], w=[rdst])
                    n += 1
        em.flush()


def phase_P2(cx, NTOK, io, ws, ident, rident, last):
    em = cx.em
    HALO = 128
    tiles = [(0, HALO, True)] + [(HALO + i * 512, 512, False) for i in range(NTOK // 512)]
    with ExitStack() as st:
        h, rh = cx.sb(st, "h", [128, 4, D], F32)
        hT, rhT = cx.sb(st, "hT", [128, KC, 512], BF16)
        big, _ = cx.sb(st, "big", [128, 20480], BF16)
        mT = big[:, 0:KC * 512].rearrange("p (k t) -> p k t", k=KC)
        rmT = Res("mT")
        brT = [big[:, (16 + 8 * i) * 512:(24 + 8 * i) * 512].rearrange("p (k t) -> p k t", k=8) for i in range(3)]
        rbr = [Res(f"br{i}") for i in range(3)]
        hid = big[:, 0:22 * 512].rearrange("p (k t) -> p k t", k=22)
        rhid = Res("hid")
        wb = [cx.sb(st, f"wb{i}", [128, 16, 256], BF16) for i in range(7)]
        nmwb, rnm = cx.sb(st, "nmwb", [128, D], F32)
        nfwb, rnf = cx.sb(st, "nfwb", [128, D], F32)
        nbufs = (cx.sb(st, "junk", [128, D], BF16), cx.sb(st, "ss", [128, 4], F32), cx.sb(st, "hb", [128, D], BF16))
        tps = [cx.ps(st, f"tp{i}", [128, 8, 128], BF16) for i in range(2)]
        accs = [cx.ps(st, f"acc{i}", [128, 512], F32) for i in range(6)]
        ft = {nm: cx.sb(st, nm, [128, 512], F32) for nm in ("g0", "g1", "g2", "sg", "zv", "yb", "yc")}
        tmp = [cx.sb(st, f"tmp{i}", [128, 256], F32) for i in range(2)]
        ubv, rubv = cx.sb(st, "ubv", [128, 516], F32)
        ubg, rubg = cx.sb(st, "ubg", [128, 516], F32)
        av, rav = cx.sb(st, "av", [128, 512], F32)
        ag, rag = cx.sb(st, "ag", [128, 512], F32)
        halo_v, rhv = cx.sb(st, "halo_v", [128, NFC, 2], F32)
        halo_g, rhg = cx.sb(st, "halo_g", [128, NFC, 2], F32)
        cw, rcw = cx.sb(st, "cw", [128, 2 * NFC, 4], F32)
        em.dma("sp", nmwb[:], io["nmwb"], w=[rnm])
        em.dma("sp", nfwb[:], io["nfwb"], w=[rnf])
        em.dma("sp", cw[:], io["cw"], w=[rcw])
        em.op("pool", lambda e: e.memset(halo_v[:], 0.0), w=[rhv])
        em.op("pool", lambda e: e.memset(halo_g[:], 0.0), w=[rhg])
        if last:
            nlw, rnl = cx.sb(st, "nlw", [128, D], F32)
            em.dma("sp", nlw[:], io["nlwb"], w=[rnl])
        cnt = {"acc": 0, "wb": 0, "tmp": 0}

        def next_acc():
            a = accs[cnt["acc"] % len(accs)]
            cnt["acc"] += 1
            return a

        def load_w(name, nb, k0, k1):
            w_, rw_ = wb[cnt["wb"] % len(wb)]
            cnt["wb"] += 1
            src, rsrc = ws[name]
            em.dma("sp", w_[:, 0:k1 - k0, :], src[nb, :, k0:k1, :], r=[rsrc], w=[rw_])
            return w_, rw_

        def resid_add(a, ra, s, c0):
            t_, rt_ = tmp[cnt["tmp"] % 2]
            cnt["tmp"] += 1
            em.op("act", lambda e: e.copy(out=t_[:, :], in_=a[:, 0:256]), r=[ra], w=[rt_])
            em.op("dve", lambda e: e.tensor_tensor(out=h[:, s, c0:c0 + 256], in0=h[:, s, c0:c0 + 256], in1=t_[:, :], op=ALU.add),
                  r=[rh, rt_], w=[rh])

        def do_tile(row0, ntok, halo_only):
            nsub = ntok // 128
            T_ = slice(0, ntok)
            for s in range(nsub):
                em.dma("sp", h[:, s, :], io["hx"][row0 + s * 128:row0 + (s + 1) * 128, :], w=[rh])
            for i, nm in enumerate(("yaTx", "obTx", "ocTx")):
                em.dma("sp", brT[i][:, :, T_], io[nm].rearrange("(k p) t -> p k t", p=128)[:, :, row0:row0 + ntok],
                       w=[rbr[i], rhid])
            for s in range(nsub):
                norm_transpose(cx, h[:, s, :], rh, hT, rhT, s * 128, nmwb, rnm, nbufs + (tps,), ident, rident)
            if DBG.get('stop') == 'A':
                return
            for cp in range(8):
                wg_ = [load_w("wg", br * 8 + cp, 0, 16) for br in range(3)]
                wglu_v = load_w("wglu", cp, 0, 8)
                wglu_g = load_w("wglu", 8 + cp, 0, 8)
                wbg = load_w("wbrg", cp, 0, 8)
                wbd = load_w("wbrd", cp, 0, 8)
                for hf in range(2):
                    c = 2 * cp + hf
                    cols = slice(hf * 128, hf * 128 + 128)

                    def proj(wt, nk, rhs, rrhs, dst, func):
                        w_, rw_ = wt
                        a, ra = next_acc()
                        for k in range(nk):
                            em.op("pe", lambda e, a=a, w_=w_, k=k, cols=cols: e.matmul(a[:, T_], w_[:, k, cols], rhs[:, k, T_],
                                                                                      start=(k == 0), stop=(k == nk - 1)),
                                  r=[rw_, rrhs], w=[ra])
                        d_, rd_ = ft[dst]
                        if func is None:
                            em.op("act", lambda e, a=a, d_=d_: e.copy(out=d_[:, T_], in_=a[:, T_]), r=[ra], w=[rd_])
                        else:
                            em.op("act", lambda e, a=a, d_=d_: e.activation(out=d_[:, T_], in_=a[:, T_], func=func), r=[ra], w=[rd_])

                    for br in range(3):
                        proj(wg_[br], KC, hT, rhT, f"g{br}", AF.Sigmoid)
                    proj(wglu_v, 8, brT[0], rbr[0], "zv", None)
                    proj(wglu_g, 8, brT[0], rbr[0], "sg", AF.Sigmoid)
                    proj(wbg, 8, brT[1], rbr[1], "yb", None)
                    proj(wbd, 8, brT[2], rbr[2], "yc", None)
                    f_ = {k_: v_[0] for k_, v_ in ft.items()}
                    r_ = {k_: v_[1] for k_, v_ in ft.items()}
                    em.op("dve", lambda e, f_=f_: e.tensor_tensor(out=f_["zv"][:, T_], in0=f_["zv"][:, T_], in1=f_["sg"][:, T_], op=ALU.mult),
                          r=[r_["zv"], r_["sg"]], w=[r_["zv"]])
                    em.op("dve", lambda e, f_=f_: e.tensor_tensor(out=f_["zv"][:, T_], in0=f_["zv"][:, T_], in1=f_["g0"][:, T_], op=ALU.mult),
                          r=[r_["zv"], r_["g0"]], w=[r_["zv"]])
                    em.op("pool", lambda e, f_=f_: e.tensor_tensor(out=f_["yb"][:, T_], in0=f_["yb"][:, T_], in1=f_["g1"][:, T_], op=ALU.mult),
                          r=[r_["yb"], r_["g1"]], w=[r_["yb"]])
                    em.op("pool", lambda e, f_=f_: e.tensor_tensor(out=f_["yc"][:, T_], in0=f_["yc"][:, T_], in1=f_["g2"][:, T_], op=ALU.mult),
                          r=[r_["yc"], r_["g2"]], w=[r_["yc"]])
                    em.op("dve", lambda e, f_=f_: e.tensor_tensor(out=f_["zv"][:, T_], in0=f_["zv"][:, T_], in1=f_["yb"][:, T_], op=ALU.add),
                          r=[r_["zv"], r_["yb"]], w=[r_["zv"]])
                    em.op("dve", lambda e, f_=f_, c=c: e.tensor_tensor(out=mT[:, c, T_], in0=f_["zv"][:, T_], in1=f_["yc"][:, T_], op=ALU.add),
                          r=[r_["zv"], r_["yc"]], w=[rmT, rhid])
            if DBG.get('stop') == 'B':
                return
            for nb in range(8):
                wo, rwo = load_w("wout", nb, 0, 16)
                for s in range(nsub):
                    a, ra = next_acc()
                    for c in range(KC):
                        em.op("pe", lambda e, a=a, wo=wo, c=c, s=s: e.matmul(a[:, 0:256], mT[:, c, s * 128:(s + 1) * 128], wo[:, c, :],
                                                                            start=(c == 0), stop=(c == KC - 1)), r=[rmT, rwo], w=[ra])
                    resid_add(a, ra, s, nb * 256)
            if DBG.get('stop') == 'C':
                return
            for s in range(nsub):
                norm_transpose(cx, h[:, s, :], rh, hT, rhT, s * 128, nfwb, rnf, nbufs + (tps,), ident, rident)
            if DBG.get('stop') == 'D':
                return
            wcache = {}
            for (c0, c1) in ((0, 22), (22, NFC)):
                for cc in range(c0, c1):
                    def getw(q):
                        blk = q // 2
                        if blk not in wcache:
                            if len(wcache) >= 3:
                                wcache.pop(next(iter(wcache)))
                            wcache[blk] = load_w("wup", blk, 0, 16)
                        return wcache[blk], slice((q % 2) * 128, (q % 2) * 128 + 128)

                    for (q, ub_, rub_, hal, rhal) in ((cc, ubv, rubv, halo_v, rhv), (NFC + cc, ubg, rubg, halo_g, rhg)):
                        (w_, rw_), cols = getw(q)
                        a, ra = next_acc()
                        for k in range(KC):
                            em.op("pe", lambda e, a=a, w_=w_, k=k, cols=cols: e.matmul(a[:, T_], w_[:, k, cols], hT[:, k, T_], start=(k == 0),
                                                                                      stop=(k == KC - 1)), r=[rw_, rhT], w=[ra])
                        em.op("act", lambda e, a=a, ub_=ub_: e.copy(out=ub_[:, 2:2 + ntok], in_=a[:, T_]), r=[ra], w=[rub_])
                        if DBG.get("e_stop") == "mm":
                            continue
                        if not halo_only:
                            em.op("dve", lambda e, ub_=ub_, hal=hal, cc=cc: e.tensor_copy(out=ub_[:, 0:2], in_=hal[:, cc, :]), r=[rhal], w=[rub_])
                        em.op("dve", lambda e, ub_=ub_, hal=hal, cc=cc: e.tensor_copy(out=hal[:, cc, :], in_=ub_[:, ntok:ntok + 2]),
                              r=[rub_], w=[rhal])
                        if halo_only or DBG.get("e_stop") == "halo":
                            continue
                        acc_, racc_ = (av, rav) if ub_ is ubv else (ag, rag)
                        em.op("dve", lambda e, ub_=ub_, acc_=acc_, q=q: e.tensor_scalar(
                            out=acc_[:, T_], in0=ub_[:, 2:2 + ntok], scalar1=cw[:, q, 2:3], scalar2=cw[:, q, 3:4], op0=ALU.mult,
                            op1=ALU.add), r=[rub_, rcw], w=[racc_])
                        em.op("dve", lambda e, ub_=ub_, acc_=acc_, q=q: e.scalar_tensor_tensor(
                            out=acc_[:, T_], in0=ub_[:, 1:1 + ntok], scalar=cw[:, q, 1:2], in1=acc_[:, T_], op0=ALU.mult, op1=ALU.add),
                            r=[rub_, rcw, racc_], w=[racc_])
                        em.op("dve", lambda e, ub_=ub_, acc_=acc_, q=q: e.scalar_tensor_tensor(
                            out=acc_[:, T_], in0=ub_[:, 0:ntok], scalar=cw[:, q, 0:1], in1=acc_[:, T_], op0=ALU.mult, op1=ALU.add),
                            r=[rub_, rcw, racc_], w=[racc_])
                    if halo_only or DBG.get("e_stop") in ("mm", "halo", "conv"):
                        continue
                    em.op("act", lambda e: e.activation(out=ag[:, T_], in_=ag[:, T_], func=AF.Silu), r=[rag], w=[rag])
                    em.op("pool", lambda e, cc=cc, c0=c0: e.tensor_tensor(out=hid[:, cc - c0, T_], in0=ag[:, T_], in1=av[:, T_], op=ALU.mult),
                          r=[rag, rav], w=[rhid, rmT, rbr[0], rbr[1], rbr[2]])
                if halo_only or DBG.get("e_stop"):
                    continue
                for nb in range(8):
                    cm = c0 + 11
                    wdA, rwdA = load_w("wdn", nb, c0, cm)
                    wdB, rwdB = load_w("wdn", nb, cm, c1)
                    for s in range(nsub):
                        a, ra = next_acc()
                        for cc in range(c0, c1):
                            wd, rwd, off = (wdA, rwdA, c0) if cc < cm else (wdB, rwdB, cm)
                            em.op("pe", lambda e, a=a, wd=wd, cc=cc, s=s, c0=c0, off=off, last_=(cc == c1 - 1): e.matmul(
                                a[:, 0:256], hid[:, cc - c0, s * 128:(s + 1) * 128], wd[:, cc - off, :], start=(cc == c0),
                                stop=last_), r=[rhid, rwd], w=[ra])
                        resid_add(a, ra, s, nb * 256)
                wcache.clear()
            if halo_only:
                return
            orow = row0 - HALO
            for s in range(nsub):
                if last:
                    (junk, rjunk), (ss, rss), (hb, rhb) = nbufs
                    em.op("act", lambda e, s=s: e.activation(out=junk[:], in_=h[:, s, :], func=AF.Square, accum_out=ss[:, 0:1]),
                          r=[rh], w=[rjunk, rss])
                    em.op("act", lambda e: e.activation(out=ss[:, 1:2], in_=ss[:, 0:1], func=AF.Sqrt, scale=1.0 / D, bias=EPS),
                          r=[rss], w=[rss])
                    em.op("dve", lambda e: e.reciprocal(out=ss[:, 2:3], in_=ss[:, 1:2]), r=[rss], w=[rss])
                    em.op("dve", lambda e, s=s: e.scalar_tensor_tensor(out=h[:, s, :], in0=h[:, s, :], scalar=ss[:, 2:3], in1=nlw[:],
                                                                      op0=ALU.mult, op1=ALU.mult), r=[rh, rss, rnl], w=[rh])
                em.dma("sp", io["hout"][orow + s * 128:orow + (s + 1) * 128, :], h[:, s, :], r=[rh], w=[io["hout_res"]])

        for (row0, ntok, halo_only) in tiles:
            do_tile(row0, ntok, halo_only)
        em.flush()


ALIBI_SLOPES = tuple(2.0 ** (-8.0 * (h + 1) / 4) for h in range(4))
P2_WSPEC = {"wg": (2048, 6144), "wglu": (1024, 4096), "wbrg": (1024, 2048), "wbrd": (1024, 2048), "wout": (2048, 2048),
            "wup": (2048, 11008), "wdn": (5504, 2048)}


def build_M(S):
    nc = bass.Bass("TRN2", target_bir_lowering=False)
    with ExitStack() as st:
        cx = Ctx(nc, st)
        inp = lambda n, sh, dt=F32: nc.dram_tensor(n, list(sh), dt, kind="ExternalInput").ap()
        out = lambda n, sh, dt: nc.dram_tensor(n, list(sh), dt, kind="ExternalOutput").ap()
        io = {"h": inp("h", [S, D]), "wm": inp("wm", [D, NCOL_M]), "nmwb": inp("nmwb", [128, D])}
        for k in ("a_re", "a_im", "ldt"):
            io[k] = inp(k, [128, 8])
        for k in ("b_re", "b_im", "c_re", "c_im"):
            io[k] = inp(k, [128, 8, 16])
        io["dsk"] = inp("dsk", [128, 2])
        io.update({"wg": inp("wg", [16, 128]), "bg": inp("bg", [128, 1]), "gnw": inp("gnw", [128, 256])})
        io.update({"al": inp("al", [3, 128]), "ar": inp("ar", [3, 512]), "ctab": inp("ctab", [128, 132]),
                   "lq1": inp("lq1", [128, 128]), "lk1": inp("lk1", [128, 128]), "lq2": inp("lq2", [128, 128]),
                   "lk2": inp("lk2", [128, 128]), "lconst": inp("lconst", [128, 2]), "dnw": inp("dnw", [128, 256])})
        for k in ("yaT", "obT", "ocT"):
            io[k] = out(k, [256, S], BF16)
            io[k + "_res"] = Res(k)
        scr = {"uT": cx.dram("uT", [256, S], F32), "qgT": cx.dram("qgT", [128, S], F32), "kgT": cx.dram("kgT", [128, S], F32),
               "qdT": cx.dram("qdT", [2, 128, S], BF16), "kdT": cx.dram("kdT", [2, 128, S], BF16),
               "alrT": cx.dram("alrT", [16, S], F32), "vg": cx.dram("vg", [S, 256], BF16), "rg": cx.dram("rg", [S, 256], F32),
               "vd": cx.dram("vd", [S, 256], BF16)}
        ident, rident = make_ident(cx, st)
        phase_M1(cx, S, io, scr, ident, rident)
        phase_M2(cx, S, io, scr, ident, rident)
        phase_M3(cx, S, io, scr, ident, rident)
        phase_M4(cx, S, io, scr, ident, rident)
    return nc


def build_P(NTOK, last):
    nc = bass.Bass("TRN2", target_bir_lowering=False)
    R = NTOK + 128
    with ExitStack() as st:
        cx = Ctx(nc, st)
        inp = lambda n, sh, dt=F32: nc.dram_tensor(n, list(sh), dt, kind="ExternalInput").ap()
        io = {"hx": inp("hx", [R, D]), "yaTx": inp("yaTx", [1024, R], BF16), "obTx": inp("obTx", [1024, R], BF16),
              "ocTx": inp("ocTx", [1024, R], BF16), "nmwb": inp("nmwb", [128, D]), "nfwb": inp("nfwb", [128, D]),
              "cw": inp("cw", [128, 2 * NFC, 4]),
              "hout": nc.dram_tensor("hout", [NTOK, D], F32, kind="ExternalOutput").ap(), "hout_res": Res("hout")}
        if last:
            io["nlwb"] = inp("nlwb", [128, D])
        wl, ws = [], {}
        for nm, (k_, n_) in P2_WSPEC.items():
            src = inp(nm, [k_, n_])
            dst, rdst = cx.dram(nm + "_s", [n_ // 256, 128, k_ // 128, 256], BF16)
            wl.append((src, dst, rdst))
            ws[nm] = (dst, rdst)
        ident, rident = make_ident(cx, st)
        phase_P0(cx, wl)
        phase_P2(cx, NTOK, io, ws, ident, rident, last)
    return nc


def _bc(x, n=128):
    x = np.asarray(x, np.float32)
    return np.ascontiguousarray(np.broadcast_to(x, (n, x.shape[-1])))


def kernel(x, norm_mix_w, w_in, ssm_a_re, ssm_a_im, ssm_log_dt, ssm_b_re, ssm_b_im, ssm_c_re, ssm_c_im, ssm_d, ssm_w_glu,
           gla_w_gate, gla_b_gate, gla_norm_w, gla_w_br, diff_lam_q1, diff_lam_k1, diff_lam_q2, diff_lam_k2, diff_norm_w,
           diff_w_br, w_out, norm_ffn_w, ffn_w_up, ffn_conv_w, ffn_conv_b, ffn_w_down, norm_final_w):
    f = lambda a: np.asarray(a, dtype=np.float32)
    x = f(x)
    B, S, _ = x.shape
    DEPTH = w_in.shape[0]
    NTOK = S // 4
    h = x
    ncM = build_M(S)
    cores = list(range(8))
    for l in range(DEPTH):
        linit = 0.8 - 0.6 * math.exp(-0.3 * l)
        W = f(w_in[l])
        in_maps = []
        for core in cores:
            b, j = core // 4, core % 4
            wm = np.concatenate([W[:, 256 * j:256 * j + 256], W[:, 1024 + 128 * j:1024 + 128 * j + 128],
                                 W[:, 1536 + 128 * j:1536 + 128 * j + 128], W[:, 4112 + 256 * j:4112 + 256 * j + 256],
                                 W[:, 5136 + 256 * j:5136 + 256 * j + 256], W[:, 2048 + 256 * j:2048 + 256 * j + 256],
                                 W[:, 3072 + 256 * j:3072 + 256 * j + 256], W[:, 6160 + 256 * j:6160 + 256 * j + 256],
                                 W[:, 4096:4112]], axis=1)
            m = {"h": np.ascontiguousarray(h[b]), "wm": np.ascontiguousarray(wm), "nmwb": _bc(norm_mix_w[l])}
            g0 = 16 * j
            m.update(s5_layout(f(ssm_a_re[l])[g0:g0 + 16], f(ssm_a_im[l])[g0:g0 + 16], f(ssm_log_dt[l])[g0:g0 + 16],
                               f(ssm_b_re[l])[g0:g0 + 16], f(ssm_b_im[l])[g0:g0 + 16], f(ssm_c_re[l])[g0:g0 + 16],
                               f(ssm_c_im[l])[g0:g0 + 16], f(ssm_d[l])[256 * j:256 * j + 256]))
            m["wg"] = np.ascontiguousarray(f(gla_w_gate[l])[:, 128 * j:128 * j + 128])
            m["bg"] = np.ascontiguousarray(f(gla_b_gate[l])[128 * j:128 * j + 128][:, None])
            m["gnw"] = _bc(gla_norm_w[l])
            al, ar, ctab = alibi_tables(ALIBI_SLOPES[j])
            m.update({"al": al, "ar": ar, "ctab": ctab, "lq1": _bc(diff_lam_q1[l]), "lk1": _bc(diff_lam_k1[l]),
                      "lq2": _bc(diff_lam_q2[l]), "lk2": _bc(diff_lam_k2[l]),
                      "lconst": _bc(np.array([linit, 1.0 - linit], np.float32)), "dnw": _bc(diff_norm_w[l])})
            in_maps.append(m)
        res = run_bass_kernel_spmd(ncM, in_maps, core_ids=cores).results
        br = {}
        for nm in ("yaT", "obT", "ocT"):
            br[nm] = [np.concatenate([np.asarray(res[4 * b + j][nm]) for j in range(4)], axis=0) for b in range(B)]
        del res, in_maps
        last = (l == DEPTH - 1)
        ncP = build_P(NTOK, last)
        cw = np.ascontiguousarray(np.concatenate([f(ffn_conv_w[l]), f(ffn_conv_b[l])[None]], 0)
                                  .reshape(4, 2 * NFC, 128).transpose(2, 1, 0))
        shared = {"wg": np.ascontiguousarray(W[:, 7184:13328]), "wglu": f(ssm_w_glu[l]), "wbrg": f(gla_w_br[l]),
                  "wbrd": f(diff_w_br[l]), "wout": f(w_out[l]), "wup": f(ffn_w_up[l]), "wdn": f(ffn_w_down[l]),
                  "nmwb": _bc(norm_mix_w[l]), "nfwb": _bc(norm_ffn_w[l]), "cw": cw}
        if last:
            shared["nlwb"] = _bc(norm_final_w)
        in_maps = []
        for core in cores:
            b, q = core // 4, core % 4
            r0 = q * NTOK
            hx = np.zeros((NTOK + 128, D), np.float32)
            hx[128:] = h[b, r0:r0 + NTOK]
            m = dict(shared)
            for nm, key in (("yaT", "yaTx"), ("obT", "obTx"), ("ocT", "ocTx")):
                t = np.zeros((1024, NTOK + 128), ml_dtypes.bfloat16)
                t[:, 128:] = br[nm][b][:, r0:r0 + NTOK]
                if q > 0:
                    t[:, :128] = br[nm][b][:, r0 - 128:r0]
                m[key] = t
            if q > 0:
                hx[:128] = h[b, r0 - 128:r0]
            m["hx"] = hx
            in_maps.append(m)
        res = run_bass_kernel_spmd(ncP, in_maps, core_ids=cores).results
        hn = np.empty_like(h)
        for core in cores:
            b, q = core // 4, core % 4
            hn[b, q * NTOK:(q + 1) * NTOK] = np.asarray(res[core]["hout"])
        h = hn
        del res, in_maps
    return h
```
